# Optimizing a Trainium2 kernel written in Bass

```python
import jax, jax.numpy as jnp
from jax import lax
import numpy as np


D_MODEL = 1024
BATCH = 16
SEQ = 4096
DEPTH = 4

N_META = 16
HEAD_DIM = 64
N_Q_HEADS = D_MODEL // HEAD_DIM
N_KV_HEADS = N_Q_HEADS // 4
GQA_GROUP = N_Q_HEADS // N_KV_HEADS
QKV_WIDTH = (N_Q_HEADS + 2 * N_KV_HEADS) * HEAD_DIM
WINDOW = 128
BLOCK = WINDOW
ROPE_THETA = 10000.0
LRU_WIDTH = D_MODEL
LRU_BLOCKS = 4
LRU_BLOCK_W = LRU_WIDTH // LRU_BLOCKS
CONV_W = 4
LRU_C = 8.0
D_FF = 256 * (-(-(8 * D_MODEL) // (3 * 256)))
N_ATTN_LAYERS = (DEPTH + 1) // 2
N_LRU_LAYERS = DEPTH // 2
ALPHA = (2.0 * DEPTH) ** 0.25
BETA = (8.0 * DEPTH) ** -0.25
LN_EPS = 1e-5

kernel_name = 'hybrid_swa_sink_rglru_deepnorm_meta'


def _layer_norm(t, g, b):
    t32 = t.astype(jnp.float32)
    mu = jnp.mean(t32, axis=-1, keepdims=True)
    var = jnp.mean(jnp.square(t32 - mu), axis=-1, keepdims=True)
    y = (t32 - mu) * lax.rsqrt(var + LN_EPS) * g.astype(jnp.float32) + b.astype(jnp.float32)
    return y.astype(t.dtype)


def _rope(t, pos):
    half = HEAD_DIM // 2
    inv_freq = ROPE_THETA ** (-jnp.arange(half, dtype=jnp.float32) * 2.0 / HEAD_DIM)
    ang = pos[:, None] * inv_freq[None, :]
    cos = jnp.cos(ang)[None, :, None, :]
    sin = jnp.sin(ang)[None, :, None, :]
    t32 = t.astype(jnp.float32)
    t1, t2 = t32[..., :half], t32[..., half:]
    return jnp.concatenate([t1 * cos - t2 * sin, t2 * cos + t1 * sin], axis=-1).astype(t.dtype)


def _attend(q, k, v, mask, sinks):
    s = jnp.einsum('bnqkgd,bnckd->bnkgqc', q, k, preferred_element_type=jnp.float32)
    s = jnp.where(mask[None, :, None, None], s, -jnp.inf)
    sink = sinks.astype(jnp.float32)[:, :, None, None]
    m = jnp.maximum(jnp.max(s, axis=-1, keepdims=True), sink)
    e = jnp.exp(s - m)
    p = e / (jnp.sum(e, axis=-1, keepdims=True) + jnp.exp(sink - m))
    return jnp.einsum('bnkgqc,bnckd->bnqkgd', p.astype(v.dtype), v)


def _sliding_window_attention(h, w_qkv, sinks, w_o):
    B, T, _ = h.shape
    S = T - N_META
    nb = S // BLOCK
    qkv = h @ w_qkv
    q, k, v = jnp.split(qkv, [N_Q_HEADS * HEAD_DIM, (N_Q_HEADS + N_KV_HEADS) * HEAD_DIM], axis=-1)
    q = q.reshape(B, T, N_Q_HEADS, HEAD_DIM)
    k = k.reshape(B, T, N_KV_HEADS, HEAD_DIM)
    v = v.reshape(B, T, N_KV_HEADS, HEAD_DIM)
    pos = jnp.arange(T, dtype=jnp.float32)
    q = _rope(q, pos) * (HEAD_DIM ** -0.5)
    k = _rope(k, pos)
    q = q.reshape(B, T, N_KV_HEADS, GQA_GROUP, HEAD_DIM)
    sink = sinks.reshape(N_KV_HEADS, GQA_GROUP)
    km, vm = k[:, :N_META], v[:, :N_META]

    meta_mask = jnp.tril(jnp.ones((N_META, N_META), dtype=bool))[None]
    o_meta = _attend(q[:, None, :N_META], km[:, None], vm[:, None], meta_mask, sink)
    o_meta = o_meta.reshape(B, N_META, N_Q_HEADS * HEAD_DIM)

    qr = q[:, N_META:].reshape(B, nb, BLOCK, N_KV_HEADS, GQA_GROUP, HEAD_DIM)
    kr = k[:, N_META:].reshape(B, nb, BLOCK, N_KV_HEADS, HEAD_DIM)
    vr = v[:, N_META:].reshape(B, nb, BLOCK, N_KV_HEADS, HEAD_DIM)
    pad = ((0, 0), (1, 0), (0, 0), (0, 0), (0, 0))
    k_prev = jnp.pad(kr, pad)[:, :-1]
    v_prev = jnp.pad(vr, pad)[:, :-1]
    km_b = jnp.broadcast_to(km[:, None], (B, nb, N_META, N_KV_HEADS, HEAD_DIM))
    vm_b = jnp.broadcast_to(vm[:, None], (B, nb, N_META, N_KV_HEADS, HEAD_DIM))
    kb = jnp.concatenate([km_b, k_prev, kr], axis=2)
    vb = jnp.concatenate([vm_b, v_prev, vr], axis=2)
    qi = jnp.arange(BLOCK)[:, None]
    ci = jnp.arange(2 * BLOCK)[None, :]
    diff = qi + BLOCK - ci
    in_band = (diff >= 0) & (diff < WINDOW)
    blk = jnp.arange(nb)[:, None, None]
    key_exists = (blk > 0) | (ci[None] >= BLOCK)
    band_mask = in_band[None] & key_exists
    mask = jnp.concatenate([jnp.ones((nb, BLOCK, N_META), dtype=bool), band_mask], axis=-1)
    o_real = _attend(qr, kb, vb, mask, sink).reshape(B, S, N_Q_HEADS * HEAD_DIM)

    o = jnp.concatenate([o_meta, o_real], axis=1)
    return o @ w_o


def _causal_depthwise_conv(t, w, b):
    C = t.shape[-1]
    y = lax.conv_general_dilated(t, w[:, None, :].astype(t.dtype), window_strides=(1,),
                                 padding=[(CONV_W - 1, 0)],
                                 dimension_numbers=('NWC', 'WIO', 'NWC'),
                                 feature_group_count=C)
    return y + b


def _rg_lru_block(h, w_in, conv_w, conv_b, w_a, b_a, w_i, b_i, lam, w_out):
    B, T, _ = h.shape
    gate, xr = jnp.split(h @ w_in, 2, axis=-1)
    gate = jax.nn.gelu(gate)
    xr = _causal_depthwise_conv(xr, conv_w, conv_b)
    xb = xr.reshape(B, T, LRU_BLOCKS, LRU_BLOCK_W)
    r = jax.nn.sigmoid(jnp.einsum('btnc,ncd->btnd', xb, w_a) + b_a).reshape(B, T, LRU_WIDTH)
    i = jax.nn.sigmoid(jnp.einsum('btnc,ncd->btnd', xb, w_i) + b_i).reshape(B, T, LRU_WIDTH)
    log_a = LRU_C * r.astype(jnp.float32) * jax.nn.log_sigmoid(lam.astype(jnp.float32))
    a = jnp.exp(log_a)
    u = jnp.sqrt(-jnp.expm1(2.0 * log_a)) * (i * xr).astype(jnp.float32)

    def combine(left, right):
        a1, b1 = left
        a2, b2 = right
        return a1 * a2, a2 * b1 + b2

    _, hs = lax.associative_scan(combine, (a, u), axis=1)
    y = hs.astype(h.dtype) * gate
    return y @ w_out


def _swiglu(h, w_gate_up, w_down):
    g, u = jnp.split(h @ w_gate_up, 2, axis=-1)
    return (jax.nn.silu(g) * u) @ w_down


def setup_inputs(seed: int = 0) -> dict:
    key = jax.random.key(seed)
    ks = jax.random.split(key, 20)
    D = D_MODEL
    NA, NL = N_ATTN_LAYERS, N_LRU_LAYERS
    nrm = jax.random.normal
    x = nrm(ks[0], (BATCH, SEQ, D), jnp.float32)
    meta_tokens = nrm(ks[1], (N_META, D), jnp.float32)
    attn_w_qkv = nrm(ks[2], (NA, D, QKV_WIDTH), jnp.float32) * D ** -0.5
    attn_sinks = nrm(ks[3], (NA, N_Q_HEADS), jnp.float32)
    attn_w_o = nrm(ks[4], (NA, N_Q_HEADS * HEAD_DIM, D), jnp.float32) * (N_Q_HEADS * HEAD_DIM) ** -0.5 * BETA
    lru_w_in = nrm(ks[5], (NL, D, 2 * LRU_WIDTH), jnp.float32) * D ** -0.5
    lru_conv_w = nrm(ks[6], (NL, CONV_W, LRU_WIDTH), jnp.float32) * CONV_W ** -0.5
    lru_conv_b = 0.01 * nrm(ks[7], (NL, LRU_WIDTH), jnp.float32)
    lru_w_a = nrm(ks[8], (NL, LRU_BLOCKS, LRU_BLOCK_W, LRU_BLOCK_W), jnp.float32) * LRU_BLOCK_W ** -0.5
    lru_b_a = 0.01 * nrm(ks[9], (NL, LRU_BLOCKS, LRU_BLOCK_W), jnp.float32)
    lru_w_i = nrm(ks[10], (NL, LRU_BLOCKS, LRU_BLOCK_W, LRU_BLOCK_W), jnp.float32) * LRU_BLOCK_W ** -0.5
    lru_b_i = 0.01 * nrm(ks[11], (NL, LRU_BLOCKS, LRU_BLOCK_W), jnp.float32)
    a_c = jax.random.uniform(ks[12], (NL, LRU_WIDTH), jnp.float32, minval=0.9, maxval=0.999)
    a0 = a_c ** (1.0 / LRU_C)
    lru_lambda = jnp.log(a0) - jnp.log1p(-a0)
    lru_w_out = nrm(ks[13], (NL, LRU_WIDTH, D), jnp.float32) * LRU_WIDTH ** -0.5 * BETA
    ffn_w_gate_up = nrm(ks[14], (DEPTH, D, 2 * D_FF), jnp.float32) * D ** -0.5
    ffn_w_down = nrm(ks[15], (DEPTH, D_FF, D), jnp.float32) * D_FF ** -0.5 * BETA
    ln_mix_g = 1.0 + 0.05 * nrm(ks[16], (DEPTH, D), jnp.float32)
    ln_mix_b = 0.05 * nrm(ks[17], (DEPTH, D), jnp.float32)
    ln_ffn_g = 1.0 + 0.05 * nrm(ks[18], (DEPTH, D), jnp.float32)
    ln_ffn_b = 0.05 * nrm(ks[19], (DEPTH, D), jnp.float32)
    return {'x': x, 'meta_tokens': meta_tokens,
            'attn_w_qkv': attn_w_qkv, 'attn_sinks': attn_sinks, 'attn_w_o': attn_w_o,
            'lru_w_in': lru_w_in, 'lru_conv_w': lru_conv_w, 'lru_conv_b': lru_conv_b,
            'lru_w_a': lru_w_a, 'lru_b_a': lru_b_a, 'lru_w_i': lru_w_i, 'lru_b_i': lru_b_i,
            'lru_lambda': lru_lambda, 'lru_w_out': lru_w_out,
            'ffn_w_gate_up': ffn_w_gate_up, 'ffn_w_down': ffn_w_down,
            'ln_mix_g': ln_mix_g, 'ln_mix_b': ln_mix_b, 'ln_ffn_g': ln_ffn_g, 'ln_ffn_b': ln_ffn_b}


def reference(x, meta_tokens, attn_w_qkv, attn_sinks, attn_w_o,
              lru_w_in, lru_conv_w, lru_conv_b, lru_w_a, lru_b_a, lru_w_i, lru_b_i,
              lru_lambda, lru_w_out, ffn_w_gate_up, ffn_w_down,
              ln_mix_g, ln_mix_b, ln_ffn_g, ln_ffn_b):
    B = x.shape[0]
    meta = jnp.broadcast_to(meta_tokens[None].astype(x.dtype), (B, N_META, D_MODEL))
    h = jnp.concatenate([meta, x], axis=1)
    for layer in range(DEPTH):
        j = layer // 2
        if layer % 2 == 0:
            mix = _sliding_window_attention(h, attn_w_qkv[j], attn_sinks[j], attn_w_o[j])
        else:
            mix = _rg_lru_block(h, lru_w_in[j], lru_conv_w[j], lru_conv_b[j], lru_w_a[j], lru_b_a[j],
                                lru_w_i[j], lru_b_i[j], lru_lambda[j], lru_w_out[j])
        h = _layer_norm(ALPHA * h + mix, ln_mix_g[layer], ln_mix_b[layer])
        h = _layer_norm(ALPHA * h + _swiglu(h, ffn_w_gate_up[layer], ffn_w_down[layer]),
                        ln_ffn_g[layer], ln_ffn_b[layer])
    return h[:, N_META:]
```

```python
import numpy as np
from contextlib import ExitStack
import concourse.bass as bass
import concourse.mybir as mybir
from concourse.bass_utils import run_bass_kernel_spmd

F32 = mybir.dt.float32
BF16 = mybir.dt.bfloat16
AF = mybir.ActivationFunctionType
ALU = mybir.AluOpType

D = 1024
KC = 8
DFF = 2816
FC = 22
NMETA = 16
DEPTH = 4
ALPHA = (2.0 * DEPTH) ** 0.25
EPS = 1e-5
NEG = -30000.0
OFF = 4
M0 = OFF
R0 = OFF + NMETA
CH = 1024
W = R0 + CH
ORDER = [0, 4, 1, 5, 2, 6, 3, 7, 8, 12, 9, 13, 10, 14, 11, 15]
USE_SILU = True
USE_GELU_TANH = True
SKIP_MIXER = False
SKIP_FFN = False
ATT_STOP = 9


class Res:
    __slots__ = ("w", "r", "psum")

    def __init__(self, psum=False):
        self.w = None
        self.r = {}
        self.psum = psum


class Eng:
    def __init__(self, name, eng, sem):
        self.name, self.eng, self.sem, self.cnt, self.seen = name, eng, sem, 0, {}


class DSem:
    def __init__(self, key, sem):
        self.key, self.sem, self.cnt = key, sem, 0


def units(c0, c1):
    us = []
    if c0 < R0:
        us.append(0)
    for j in range(1, 9):
        a = R0 + 128 * (j - 1)
        if c0 < a + 128 and c1 > a:
            us.append(j)
    return us


class Buf:
    def __init__(self, t, nk):
        self.t = t
        self.res = [[Res() for _ in range(9)] for _ in range(nk)]

    def rs(self, k, c0, c1):
        return [self.res[k][u] for u in units(c0, c1)]

    def rsk(self, ks, c0, c1):
        return [self.res[k][u] for k in ks for u in units(c0, c1)]


class Sched:
    def __init__(self, nc, es):
        self.nc = nc
        self.es = es
        self.E = {}
        for n, a in [("pe", "tensor"), ("act", "scalar"), ("dve", "vector"), ("pool", "gpsimd"), ("sp", "sync")]:
            self.E[n] = Eng(n, getattr(nc, a), es.enter_context(nc.semaphore("s_" + n)))
        self.nds = 0

    def dsem(self):
        self.nds += 1
        return DSem("d%d" % self.nds, self.es.enter_context(self.nc.semaphore("d%d" % self.nds)))

    def _deps(self, e, reads, writes):
        best = {}
        for r in reads:
            t = r.w
            if t is not None and (t[0] not in best or best[t[0]][2] < t[2]):
                best[t[0]] = t
            if r.psum:
                for t in r.r.values():
                    if t[0] != e.name and (t[0] not in best or best[t[0]][2] < t[2]):
                        best[t[0]] = t
        for w in writes:
            t = w.w
            if t is not None and (t[0] not in best or best[t[0]][2] < t[2]):
                best[t[0]] = t
            for t in w.r.values():
                if t[0] not in best or best[t[0]][2] < t[2]:
                    best[t[0]] = t
        for t in best.values():
            if e.name == "pe" and t[0] == "pe":
                continue
            if e.seen.get(t[0], 0) >= t[2]:
                continue
            e.eng.wait_ge(t[1], t[2])
            e.seen[t[0]] = t[2]

    @staticmethod
    def _mark(tok, reads, writes):
        for r in reads:
            o = r.r.get(tok[0])
            if o is None or o[2] < tok[2]:
                r.r[tok[0]] = tok
        for w in writes:
            w.w = tok
            w.r = {}

    def op(self, en, fn, reads=(), writes=(), inc=True):
        e = self.E[en]
        self._deps(e, reads, writes)
        ins = fn(e.eng)
        if inc:
            ins.then_inc(e.sem, 1)
            e.cnt += 1
            tok = (en, e.sem, e.cnt)
        else:
            tok = (en, e.sem, e.cnt + 1)
        self._mark(tok, reads, writes)
        return ins

    def dma(self, qn, ds, parts, reads=(), writes=()):
        e = self.E[qn]
        self._deps(e, reads, writes)
        for dst, src in parts:
            e.eng.dma_start(out=dst, in_=src).then_inc(ds.sem, 16)
            ds.cnt += 16
        tok = (ds.key, ds.sem, ds.cnt)
        self._mark(tok, reads, writes)

    def snapshot(self):
        return {n: e.cnt for n, e in self.E.items()}

    def wait_snap(self, snap):
        for e in self.E.values():
            for n2, c in snap.items():
                e2 = self.E[n2]
                if e2 is e or c == 0 or e.seen.get(n2, 0) >= c:
                    continue
                e.eng.wait_ge(e2.sem, c)
                e.seen[n2] = c

    def barrier(self):
        for e in self.E.values():
            for e2 in self.E.values():
                if e2 is e or e2.cnt == 0:
                    continue
                if e.seen.get(e2.name, 0) >= e2.cnt:
                    continue
                e.eng.wait_ge(e2.sem, e2.cnt)
                e.seen[e2.name] = e2.cnt


def build(NSEQ, SEQ, layers):
    T = NMETA + SEQ
    NCH = SEQ // CH
    attn_js = sorted({l // 2 for l in layers if l % 2 == 0})
    lru_js = sorted({l // 2 for l in layers if l % 2 == 1})
    nc = bass.Bass("TRN2", target_bir_lowering=False)
    dt = lambda name, shape: nc.dram_tensor(name, shape, F32, kind="ExternalInput").ap()
    hin = dt("hin", [NSEQ, D, T])
    hout = nc.dram_tensor("hout", [NSEQ, D, T], F32, kind="ExternalOutput").ap()
    prm_d = dt("prm", [128, 272])
    cst_d = dt("cst", [128, 1920])
    rope_d = dt("rope", [128, 2, T])
    wgu_d = dt("wgu", [DEPTH, D, 2 * DFF])
    wdn_d = dt("wdn", [DEPTH, DFF, D])
    if attn_js:
        wq_d = dt("wq", [2, D, 1024])
        wkv_d = dt("wkv", [2, D, 512])
        wo_d = dt("wo", [2, D, D])
    if lru_js:
        win_d = dt("win", [2, D, 2048])
        wa_d = dt("wa", [2, 4, 256, 256])
        wi_d = dt("wi", [2, 4, 256, 256])
        wout_d = dt("wout", [2, D, D])

    with ExitStack() as es:
        S = Sched(nc, es)
        sb = lambda name, shape, d=F32: es.enter_context(nc.sbuf_tensor(name, shape, d))
        h = Buf(sb("h", [128, KC, W]), KC)
        hb = Buf(sb("hb", [128, KC, W], BF16), KC)
        act = Buf(sb("actb", [128, FC, W], BF16), FC)
        NSLOT = 3
        slots = []
        for i in range(NSLOT):
            slots.append((sb("wslot%d" % i, [128, 4096], BF16), Res(), S.dsem()))
        slot_i = [0]
        prm = sb("prm_s", [128, 272])
        prm_r = Res()
        cst = sb("cst_s", [128, 1920], BF16)
        cst_r = Res()
        ident = cst[:, 0:128]
        perm = cst[:, 128:256]
        onesD = cst[:, 256:384]
        onesh = [cst[:, 384:512], cst[:, 512:640]]
        maskb = [cst[:, 640:1152], cst[:, 1152:1664]]
        ones16 = [cst[:, 1664:1792], cst[:, 1792:1920]]
        drv = sb("drv", [128, 128])
        drv_r = Res()
        lnz = sb("lnz", [128, 2, 512])
        lnz_r = [Res(), Res()]
        lnst = sb("lnst", [128, 2, 512])
        lnst_r = [Res(), Res()]
        lnsq = sb("lnsq", [128, 4, 512], BF16)
        lnsq_r = [Res() for _ in range(4)]
        sgt, sgt_r = lnz, lnz_r
        xbr = sb("xbr", [128, 4, 512], BF16)
        xbr_r = [Res() for _ in range(4)]
        kcar = sb("kcar", [128, 2, 2, 144], BF16)
        kcar_r = [[Res(), Res()] for _ in range(2)]
        vcar = sb("vcar", [128, 2, 2, 512], BF16)
        vcar_r = [[Res(), Res()] for _ in range(2)]
        hst = sb("hst", [128, 2, KC])
        hst_r = [[Res() for _ in range(KC)] for _ in range(2)]
        chist = sb("chist", [128, 2, KC, 4], BF16)
        chist_r = [Res(), Res()]
        ARW = 14336
        arena = sb("arena", [128, ARW])
        banks = [(es.enter_context(nc.psum_tensor("ps%d" % i, [128, 512], F32)), Res(True), i) for i in range(8)]
        reserved = set()
        bank_i = [0]
        arena_snap = [None]
        ld_ds = [S.dsem() for _ in range(KC)]
        st_ds = [S.dsem() for _ in range(KC)]
        c_ds = S.dsem()
        rp_ds = S.dsem()

        def bank():
            while True:
                bank_i[0] = (bank_i[0] + 1) % 8
                if bank_i[0] not in reserved:
                    return banks[bank_i[0]]

        def bank_reserve():
            bk = bank()
            reserved.add(bk[2])
            return bk

        prefetched = []

        def issue_load(mk):
            i = slot_i[0]
            slot_i[0] = (i + 1) % NSLOT
            t, r, ds = slots[i]
            S.dma("pool", ds, mk(t), writes=[r])
            return t, r

        def load_w(mk):
            if prefetched:
                return prefetched.pop(0)
            return issue_load(mk)

        def prefetch(mk):
            prefetched.append(issue_load(mk))

        def mk_proj(w_ap, pc):
            return lambda t: [(t[:, :].rearrange("p (k n) -> p k n", k=KC),
                               w_ap[:, pc * 512:(pc + 1) * 512].rearrange("(k p) n -> p k n", p=128))]

        def mk_gu(l, pc):
            j0 = 2 * pc
            return lambda t: [
                (t[:, 0:2048].rearrange("p (k n) -> p k n", k=KC),
                 wgu_d[l, :, j0 * 128:(j0 + 2) * 128].rearrange("(k p) n -> p k n", p=128)),
                (t[:, 2048:4096].rearrange("p (k n) -> p k n", k=KC),
                 wgu_d[l, :, DFF + j0 * 128:DFF + (j0 + 2) * 128].rearrange("(k p) n -> p k n", p=128))]

        def first_pieces(kind, l):
            if kind == "attn":
                return [mk_proj(wq_d[l // 2], 0), mk_proj(wq_d[l // 2], 1)]
            if kind == "lru":
                return [mk_proj(win_d[l // 2], 0), mk_proj(win_d[l // 2], 1)]
            return [mk_gu(l, 0), mk_gu(l, 1)]

        next_phase = [None]

        def mm(bk, m, n, lhsT, rhs, start, stop, reads, inc=None):
            S.op("pe", lambda e: e.matmul(bk[0][0:m, 0:n], lhsT, rhs, start=start, stop=stop),
                 reads=reads, writes=[bk[1]], inc=(stop if inc is None else inc))

        def actf(out, in_, func, reads, writes, bias=None, scale=None):
            kw = {}
            if bias is not None:
                kw["bias"] = bias
            if scale is not None:
                kw["scale"] = scale
            S.op("act", lambda e: e.activation(out, in_, func, **kw), reads=reads, writes=writes)

        def tt(out, a, b, op, reads, writes):
            S.op("dve", lambda e: e.tensor_tensor(out, a, b, op), reads=reads, writes=writes)

        def pcol(i):
            return prm[:, i:i + 1]

        S.dma("sp", c_ds, [(prm[:], prm_d)], writes=[prm_r])
        c2_ds = S.dsem()
        S.dma("pool", c2_ds, [(cst[:], cst_d)], writes=[cst_r])
        for j in lru_js:
            lam = prm[:, 128 + j * 64 + 56:128 + j * 64 + 64]
            tmp = drv[:, 48 + j * 8:56 + j * 8]
            actf(tmp, lam, AF.Exp, [prm_r], [drv_r], scale=-1.0)
            actf(tmp, tmp, AF.Ln, [drv_r], [drv_r], bias=1.0)
            S.op("dve", lambda e: e.tensor_scalar(drv[:, j * 8:j * 8 + 8], tmp, -4.0, None, ALU.mult),
                 reads=[drv_r], writes=[drv_r])
            S.op("dve", lambda e: e.tensor_scalar(drv[:, 16 + j * 8:24 + j * 8], tmp, -8.0, None, ALU.mult),
                 reads=[drv_r], writes=[drv_r])
            S.op("dve", lambda e: e.tensor_scalar(drv[:, 64 + j * 8:72 + j * 8], prm[:, 128 + j * 64 + 40:128 + j * 64 + 48],
                                                  0.5, None, ALU.mult), reads=[prm_r, drv_r], writes=[drv_r])
            S.op("dve", lambda e: e.tensor_scalar(drv[:, 80 + j * 8:88 + j * 8], prm[:, 128 + j * 64 + 48:128 + j * 64 + 56],
                                                  0.5, None, ALU.mult), reads=[prm_r, drv_r], writes=[drv_r])
        for j in attn_js:
            actf(drv[:, 32 + j * 8:40 + j * 8], prm[:, 256 + j * 8:264 + j * 8], AF.Exp, [prm_r], [drv_r])

        tsw = sb("tsw", [128, 4])
        tsw_r = Res()
        S.op("dve", lambda e: e.memset(tsw[:, :], 0.0), writes=[tsw_r])

        def preswitch(func):
            kw = {"bias": 1.0} if func == AF.Ln else {}
            S.op("act", lambda e: e.activation(tsw[:, 1:2], tsw[:, 0:1], func, **kw), reads=[tsw_r], writes=[tsw_r])

        def ln_s1(c0, c1):
            n = c1 - c0
            bm, be = bank(), bank()
            for k in range(KC):
                hr = h.rs(k, c0, c1)
                hbr = hb.rs(k, c0, c1)
                actf(hb.t[:, k, c0:c1], h.t[:, k, c0:c1], AF.Copy, hr, hbr)
                q = k % 4
                actf(lnsq[:, q, 0:n], h.t[:, k, c0:c1], AF.Square, hr, [lnsq_r[q]])
                mm(bm, 128, n, onesD, hb.t[:, k, c0:c1], k == 0, k == KC - 1, hbr + [cst_r])
                mm(be, 128, n, onesD, lnsq[:, q, 0:n], k == 0, k == KC - 1, [lnsq_r[q], cst_r], inc=True)
            return (c0, c1, bm, be)

        def ln_s2(st, q):
            c0, c1, bm, be = st
            n = c1 - c0
            actf(lnst[:, q, 0:n], bm[0][:, 0:n], AF.Square, [bm[1]], [lnst_r[q]])
            tt(lnst[:, q, 0:n], be[0][:, 0:n], lnst[:, q, 0:n], ALU.subtract, [be[1], lnst_r[q]], [lnst_r[q]])
            actf(lnst[:, q, 0:n], lnst[:, q, 0:n], AF.Ln, [lnst_r[q]], [lnst_r[q]], bias=EPS)
            actf(be[0][:, 0:n], lnst[:, q, 0:n], AF.Exp, [lnst_r[q]], [be[1]], scale=-0.5)

        def preswitch_next():
            if next_phase[0] is None:
                return
            kind = next_phase[0][0]
            if kind == "ffn":
                preswitch(AF.Silu)
            elif kind == "lru":
                preswitch(AF.Gelu_apprx_tanh)

        def ln_s3_k(st, gi, bi, k, eng="act"):
            c0, c1, bm, br = st
            n = c1 - c0
            hr = h.rs(k, c0, c1)
            hbr = hb.rs(k, c0, c1)
            q = k % 2
            tt(lnz[:, q, 0:n], h.t[:, k, c0:c1], bm[0][:, 0:n], ALU.subtract, hr + [bm[1]], [lnz_r[q]])
            tt(lnz[:, q, 0:n], lnz[:, q, 0:n], br[0][:, 0:n], ALU.mult, [lnz_r[q], br[1]], [lnz_r[q]])
            if eng == "act":
                actf(h.t[:, k, c0:c1], lnz[:, q, 0:n], AF.Identity, [lnz_r[q], prm_r], hr,
                     bias=pcol(bi + k), scale=pcol(gi + k))
                actf(hb.t[:, k, c0:c1], lnz[:, q, 0:n], AF.Identity, [lnz_r[q], prm_r], hbr,
                     bias=pcol(bi + k), scale=pcol(gi + k))
            else:
                S.op("pool", lambda e: e.tensor_scalar(h.t[:, k, c0:c1], lnz[:, q, 0:n], pcol(gi + k), pcol(bi + k),
                                                       ALU.mult, ALU.add), reads=[lnz_r[q], prm_r], writes=hr)
                S.op("pool", lambda e: e.tensor_scalar(hb.t[:, k, c0:c1], lnz[:, q, 0:n], pcol(gi + k), pcol(bi + k),
                                                       ALU.mult, ALU.add), reads=[lnz_r[q], prm_r], writes=hbr)

        def ln_s3(st, gi, bi, split=KC):
            for k in range(KC):
                ln_s3_k(st, gi, bi, k, "act" if k < split else "pool")

        class StatAcc:
            def __init__(self, subtiles, delay):
                reserved.clear()
                self.big = [st for st in subtiles if st[1] - st[0] > 16]
                self.bk = {st: (bank_reserve(), bank_reserve()) for st in self.big}
                self.cnt = {st: 0 for st in self.big}
                self.q = []
                self.delay = delay
                self.ri = 0

            def produced(self, m, c0, c1):
                if (c0, c1) not in self.bk:
                    return
                n = c1 - c0
                sl = self.ri % 4
                self.ri += 1
                hr = h.rs(m, c0, c1)
                actf(xbr[:, sl, 0:n], h.t[:, m, c0:c1], AF.Copy, hr, [xbr_r[sl]])
                actf(lnsq[:, sl, 0:n], h.t[:, m, c0:c1], AF.Square, hr, [lnsq_r[sl]])
                self.q.append((c0, c1, sl))
                self.flush(self.delay)

            def flush(self, keep):
                while len(self.q) > keep:
                    c0, c1, sl = self.q.pop(0)
                    n = c1 - c0
                    bm, be = self.bk[(c0, c1)]
                    i = self.cnt[(c0, c1)]
                    self.cnt[(c0, c1)] = i + 1
                    mm(bm, 128, n, onesD, xbr[:, sl, 0:n], i == 0, i == KC - 1, [xbr_r[sl], cst_r], inc=True)
                    mm(be, 128, n, onesD, lnsq[:, sl, 0:n], i == 0, i == KC - 1, [lnsq_r[sl], cst_r], inc=True)

            def flush_st(self, st):
                rest = [x for x in self.q if (x[0], x[1]) != st]
                mine = [x for x in self.q if (x[0], x[1]) == st]
                self.q = mine
                self.flush(0)
                self.q = rest
                return (st[0], st[1], self.bk[st][0], self.bk[st][1])

            def finish(self):
                self.flush(0)
                return [(c0, c1, self.bk[(c0, c1)][0], self.bk[(c0, c1)][1]) for (c0, c1) in self.big]

        def layer_norm_all(subtiles, gi, bi, acc=None):
            if next_phase[0] is not None:
                for mk in first_pieces(*next_phase[0]):
                    prefetch(mk)
            big = [st for st in subtiles if st[1] - st[0] > 16]
            small = [st for st in subtiles if st[1] - st[0] <= 16]
            sts = acc.finish() if acc is not None else None
            if sts is None:
                sts = [ln_s1(c0, c1) for (c0, c1) in big]
            for i, st in enumerate(sts):
                ln_s2(st, i % 2)
            if not small:
                preswitch_next()
            for st in sts[:-1]:
                ln_s3(st, gi, bi, split=8)
            ln_s3(sts[-1], gi, bi, split=0)
            for (c0, c1) in small:
                st = ln_s1(c0, c1)
                ln_s2(st, 0)
                ln_s3(st, gi, bi)
            if small:
                preswitch_next()
            reserved.clear()
            reserved.add(sts[-1][2][2])
            reserved.add(sts[-1][3][2])

        def out_proj_ln(w_d, subtiles, src, gi, bi):
            acc = StatAcc(subtiles, 2)
            pcs = [load_w(mk_proj(w_d, 0)), load_w(mk_proj(w_d, 1))]
            big = [st for st in subtiles if st[1] - st[0] > 16]
            small = [st for st in subtiles if st[1] - st[0] <= 16]
            prev = None
            prev_eng = "act"
            for si, (c0, c1) in enumerate(big + small):
                n = c1 - c0
                for m in range(KC):
                    t, r = pcs[m // 4]
                    wv = t[:, :].rearrange("p (k n) -> p k n", k=KC)
                    mmi = m % 4
                    bk = bank()
                    for k in range(KC):
                        mm(bk, 128, n, wv[:, k, mmi * 128:(mmi + 1) * 128], src.t[:, k, c0:c1],
                           k == 0, k == KC - 1, [r] + src.rs(k, c0, c1))
                    hr = h.rs(m, c0, c1)
                    S.op("dve", lambda e: e.scalar_tensor_tensor(h.t[:, m, c0:c1], h.t[:, m, c0:c1], ALPHA,
                                                                 bk[0][:, 0:n], ALU.mult, ALU.add),
                         reads=hr + [bk[1]], writes=hr)
                    acc.produced(m, c0, c1)
                    if prev is not None:
                        ln_s3_k(prev, gi, bi, m, prev_eng)
                if (c0, c1) in big:
                    st = acc.flush_st((c0, c1))
                    ln_s2(st, si % 2)
                    last_big = (c0, c1) == big[-1]
                    if last_big and not small:
                        preswitch_next()
                    if last_big and next_phase[0] is not None:
                        fp = first_pieces(*next_phase[0])
                        prefetch(fp[0])
                        if not small:
                            prefetch(fp[1])
                    if last_big and not small:
                        ln_s3(st, gi, bi, split=0)
                        prev = None
                    else:
                        prev = st
                        prev_eng = "pool" if last_big else "act"
                else:
                    prev = None
                    st = ln_s1(c0, c1)
                    ln_s2(st, 0)
                    ln_s3(st, gi, bi)
                    preswitch_next()
            if small and next_phase[0] is not None:
                prefetch(first_pieces(*next_phase[0])[1])
            lb = big[-1]
            reserved.clear()
            reserved.add(acc.bk[lb][0][2])
            reserved.add(acc.bk[lb][1][2])


        def ffn(l, subtiles):
            for pc in range(FC // 2):
                j0 = 2 * pc
                t, r = load_w(mk_gu(l, pc))
                wg = t[:, 0:2048].rearrange("p (k n) -> p k n", k=KC)
                wu = t[:, 2048:4096].rearrange("p (k n) -> p k n", k=KC)
                for (c0, c1) in subtiles:
                    n = c1 - c0
                    for jj in range(2):
                        j = j0 + jj
                        bg, bu = bank(), bank()
                        for k in range(KC):
                            mm(bg, 128, n, wg[:, k, jj * 128:(jj + 1) * 128], hb.t[:, k, c0:c1],
                               k == 0, k == KC - 1, [r] + hb.rs(k, c0, c1))
                        for k in range(KC):
                            mm(bu, 128, n, wu[:, k, jj * 128:(jj + 1) * 128], hb.t[:, k, c0:c1],
                               k == 0, k == KC - 1, [r] + hb.rs(k, c0, c1))
                        q = j % 2
                        ar = act.rs(j, c0, c1)
                        if USE_SILU:
                            actf(sgt[:, q, 0:n], bg[0][:, 0:n], AF.Silu, [bg[1]], [sgt_r[q]])
                        else:
                            actf(sgt[:, q, 0:n], bg[0][:, 0:n], AF.Sigmoid, [bg[1]], [sgt_r[q]])
                            tt(sgt[:, q, 0:n], sgt[:, q, 0:n], bg[0][:, 0:n], ALU.mult, [sgt_r[q], bg[1]], [sgt_r[q]])
                        tt(act.t[:, j, c0:c1], sgt[:, q, 0:n], bu[0][:, 0:n], ALU.mult, [sgt_r[q], bu[1]], ar)
                reserved.clear()
            preswitch(AF.Ln)
            acc = StatAcc(subtiles, 1)
            for m in range(KC):
                t, r = load_w(lambda t: [(t[:, 0:FC * 128].rearrange("p (j n) -> p j n", j=FC),
                                          wdn_d[l, :, m * 128:(m + 1) * 128].rearrange("(j p) n -> p j n", p=128))])
                wd = t[:, 0:FC * 128].rearrange("p (j n) -> p j n", j=FC)
                for (c0, c1) in subtiles:
                    n = c1 - c0
                    bk = bank()
                    for j in range(FC):
                        mm(bk, 128, n, wd[:, j, :], act.t[:, j, c0:c1], j == 0, j == FC - 1,
                           [r] + act.rs(j, c0, c1))
                    hr = h.rs(m, c0, c1)
                    S.op("dve", lambda e: e.scalar_tensor_tensor(h.t[:, m, c0:c1], h.t[:, m, c0:c1], ALPHA,
                                                                 bk[0][:, 0:n], ALU.mult, ALU.add),
                         reads=hr + [bk[1]], writes=hr)
                    acc.produced(m, c0, c1)
            layer_norm_all(subtiles, l * 32 + 16, l * 32 + 24, acc)

        def attention(l, c, subtiles):
            j = l // 2
            if arena_snap[0] is not None:
                S.wait_snap(arena_snap[0])
            a16 = arena[:, 0:6400].bitcast(BF16)
            kT = a16[:, 0:2048].rearrange("p (k n) -> p k n", k=2)
            kT_r = [[Res() for _ in range(8)] for _ in range(2)]
            Vz = a16[:, 2048:6144].rearrange("p (b n) -> p b n", b=8)
            Vz_r = [Res() for _ in range(8)]
            Et = a16[:, 6144:9216].rearrange("p (s n) -> p s n", s=6)
            Et_r = [Res() for _ in range(6)]
            qb = a16[:, 9216:10240].rearrange("p (s n) -> p s n", s=2)
            qb_r = [Res(), Res()]
            t12 = arena[:, 6400:8448].rearrange("p (s n) -> p s n", s=4)
            t12_r = [Res() for _ in range(4)]
            rp = arena[:, 8448:8448 + 2 * W].rearrange("p (s n) -> p s n", s=2)
            rp_r = Res()
            dn = arena[:, 10560:11584].rearrange("p (s n) -> p s n", s=2)
            dn_r = [Res(), Res()]
            cfirst = min(st_[0] for st_ in subtiles)
            pos0 = 0 if c == 0 else NMETA + CH * c
            ncol = W - cfirst
            S.dma("sp", rp_ds, [(rp[:, 0, cfirst:W], rope_d[:, 0, pos0:pos0 + ncol]),
                                 (rp[:, 1, cfirst:W], rope_d[:, 1, pos0:pos0 + ncol])], writes=[rp_r])
            S.op("pool", lambda e: e.memset(arena[:, 1024:3072], 0.0), writes=Vz_r)
            S.op("pool", lambda e: e.memset(arena[:, 3072:4608], 0.0), writes=Et_r)
            if c == 0:
                S.op("dve", lambda e: e.memset(vcar[:, j, :, :], 0.0), writes=vcar_r[j])
                S.op("dve", lambda e: e.memset(kcar[:, j, :, :], 0.0), writes=kcar_r[j])
            if ATT_STOP <= 0:
                S.op('dve', lambda e: e.tensor_copy(t12[:, 0, 0:16], rp[:, 0, cfirst:cfirst + 16]), reads=[rp_r], writes=[t12_r[0]])
                return
            rcnt = [0]

            def rope_out(bk, n, c0, c1, out_ap, out_res):
                q = rcnt[0] % 2
                rcnt[0] += 1
                actf(qb[:, q, 0:n], bk[0][:, 0:n], AF.Copy, [], [qb_r[q], bk[1]])
                b2 = bank()
                mm(b2, 128, n, perm, qb[:, q, 0:n], True, True, [qb_r[q], cst_r])
                tt(t12[:, 2 * q, 0:n], rp[:, 0, c0:c1], bk[0][:, 0:n], ALU.mult, [bk[1], rp_r], [t12_r[2 * q]])
                tt(t12[:, 2 * q + 1, 0:n], rp[:, 1, c0:c1], b2[0][:, 0:n], ALU.mult, [b2[1], rp_r], [t12_r[2 * q + 1]])
                tt(out_ap, t12[:, 2 * q, 0:n], t12[:, 2 * q + 1, 0:n], ALU.add,
                   [t12_r[2 * q], t12_r[2 * q + 1]], out_res)

            for pc in range(2):
                t, r = load_w(mk_proj(wq_d[j], pc))
                wv = t[:, :].rearrange("p (k n) -> p k n", k=KC)
                for (c0, c1) in subtiles:
                    n = c1 - c0
                    for cc in range(4):
                        ch = pc * 4 + cc
                        bk = bank()
                        for k in range(KC):
                            mm(bk, 128, n, wv[:, k, cc * 128:(cc + 1) * 128], hb.t[:, k, c0:c1],
                               k == 0, k == KC - 1, [r] + hb.rs(k, c0, c1))
                        rope_out(bk, n, c0, c1, act.t[:, ch, c0:c1], act.rs(ch, c0, c1))
                reserved.clear()
            if ATT_STOP <= 1:
                return
            t, r = load_w(lambda t: [(t[:, :].rearrange("p (k n) -> p k n", k=KC),
                                      wkv_d[j].rearrange("(k p) n -> p k n", p=128))])
            wv = t[:, :].rearrange("p (k n) -> p k n", k=KC)
            for (c0, c1) in subtiles:
                n = c1 - c0
                meta = c0 < R0
                for kc in range(2):
                    bk = bank()
                    for k in range(KC):
                        mm(bk, 128, n, wv[:, k, kc * 128:(kc + 1) * 128], hb.t[:, k, c0:c1],
                           k == 0, k == KC - 1, [r] + hb.rs(k, c0, c1))
                    if meta:
                        rope_out(bk, n, c0, c1, kcar[:, j, kc, 0:16], [kcar_r[j][0]])
                    else:
                        b0 = (c0 - R0) // 128
                        rope_out(bk, n, c0, c1, kT[:, kc, c0 - R0:c1 - R0],
                                 [kT_r[kc][b] for b in range(b0, b0 + n // 128)])
                for u in units(c0, c1):
                    if u == 0:
                        ua, nt = M0, 16
                    else:
                        ua, nt = R0 + 128 * (u - 1), 128
                    bk = bank()
                    for k in range(KC):
                        mm(bk, 128, 256, hb.t[:, k, ua:ua + 128], wv[:, k, 256:512], k == 0, k == KC - 1,
                           [r] + hb.rs(k, ua, ua + 128))
                    if u == 0:
                        dst, dres = vcar[0:16, j, 0, :], [vcar_r[j][0]]
                    else:
                        dst, dres = Vz[:, u - 1, :], [Vz_r[u - 1]]
                    src = bk[0][0:nt, 0:256].rearrange("p (g d) -> p g d", g=4)
                    d4 = dst.rearrange("p (g x) -> p g x", g=4)
                    actf(d4[:, 0::2, 0:64], src[:, 0::2, :], AF.Copy, [bk[1]], dres)
                    actf(d4[:, 1::2, 64:128], src[:, 1::2, :], AF.Copy, [bk[1]], dres)

            if ATT_STOP <= 2:
                return
            prefetch(mk_proj(wo_d[j], 0))
            prefetch(mk_proj(wo_d[j], 1))
            items = []
            ulist = ([0] if c == 0 else []) + list(range(1, 9))
            for u in ulist:
                for pair in range(2):
                    for half in range(2):
                        items.append((u, pair, half))

            def keygroups(u, pair, half):
                g = 2 * pair + half
                lo, hi = 64 * half, 64 * half + 64
                if u == 0:
                    return [(kcar[lo:hi, j, pair, 0:128], kcar_r[j], vcar[:, j, 0, g * 128:(g + 1) * 128],
                             [vcar_r[j][0]], 16, 1)]
                b = u - 1
                kg = [(kcar[lo:hi, j, pair, 0:128], kcar_r[j], vcar[:, j, 0, g * 128:(g + 1) * 128],
                       [vcar_r[j][0]], 16, None)]
                if b == 0:
                    if c > 0:
                        kg.append((kcar[lo:hi, j, pair, 16:144], [kcar_r[j][1]],
                                   vcar[:, j, 1, g * 128:(g + 1) * 128], [vcar_r[j][1]], 128, 0))
                else:
                    kg.append((kT[lo:hi, pair, (b - 1) * 128:b * 128], [kT_r[pair][b - 1]],
                               Vz[:, b - 1, g * 128:(g + 1) * 128], [Vz_r[b - 1]], 128, 0))
                kg.append((kT[lo:hi, pair, b * 128:(b + 1) * 128], [kT_r[pair][b]],
                           Vz[:, b, g * 128:(g + 1) * 128], [Vz_r[b]], 128, 1))
                return kg

            ecnt = [0]
            mcnt = [0]

            def scores_mm(item):
                u, pair, half = item
                if u == 0:
                    ua, nq = M0, 16
                else:
                    ua, nq = R0 + 128 * (u - 1), 128
                N = 4 * nq
                lo, hi = 64 * half, 64 * half + 64
                qap = act.t[lo:hi, 4 * pair:4 * pair + 4, ua:ua + nq]
                qres = act.rsk(range(4 * pair, 4 * pair + 4), ua, ua + nq)
                out = []
                for (kap, kres, vap, vres, nk, mk) in keygroups(u, pair, half):
                    bk = bank()
                    mm(bk, 128, N, kap, qap, True, True, kres + qres)
                    out.append((bk, nk, mk, vap, vres))
                return out

            def scores_exp(item, sm):
                u, pair, half = item
                nq = 16 if u == 0 else 128
                N = 4 * nq
                out = []
                for (bk, nk, mk, vap, vres) in sm:
                    if nk == 16:
                        s = 4 + mcnt[0] % 2
                        mcnt[0] += 1
                    else:
                        s = ecnt[0] % 4
                        ecnt[0] += 1
                    actf(Et[0:nk, s, 0:N], bk[0][0:nk, 0:N], AF.Exp, [bk[1]], [Et_r[s]], scale=0.125)
                    if mk is not None:
                        mrhs = maskb[mk][0:nk, :].rearrange("p (a b) -> p a b", a=4)[:, :, 0:nq]
                        e3 = Et[0:nk, s, 0:N].rearrange("p (a b) -> p a b", a=4)
                        tt(e3, e3, mrhs, ALU.mult, [Et_r[s], cst_r], [Et_r[s]])
                    out.append((s, nk, vap, vres))
                return out

            pv_state = {}

            def pv(item, ex):
                u, pair, half = item
                if u == 0:
                    ua, nq = M0, 16
                else:
                    ua, nq = R0 + 128 * (u - 1), 128
                N = 4 * nq
                if half == 0:
                    pv_state["o"], pv_state["d"] = bank(), bank()
                bo, bd = pv_state["o"], pv_state["d"]
                for i, (s, nk, vap, vres) in enumerate(ex):
                    first = (half == 0 and i == 0)
                    last = (half == 1 and i == len(ex) - 1)
                    mm(bo, 128, N, vap, Et[:, s, 0:N], first, last, vres + [Et_r[s]])
                    mm(bd, 128, N, (ones16 if nk == 16 else onesh)[half], Et[:, s, 0:N], first, last, [cst_r, Et_r[s]],
                       inc=(i == len(ex) - 1))
                if half == 1:
                    return (u, pair, bo, bd)
                return None

            ncnt = [0]

            def norm(u, pair, bo, bd):
                if u == 0:
                    ua, nq = M0, 16
                else:
                    ua, nq = R0 + 128 * (u - 1), 128
                N = 4 * nq
                q = ncnt[0] % 2
                ncnt[0] += 1
                sk = drv[:, 32 + j * 8 + pair * 4:32 + j * 8 + pair * 4 + 4]
                skb = bass.AP(sk.tensor, sk.offset, [list(sk.ap[0]), [1, 4], [0, nq]])
                d3 = dn[:, q, 0:N].rearrange("p (a b) -> p a b", a=4)
                tt(d3, skb, bd[0][:, 0:N].rearrange("p (a b) -> p a b", a=4), ALU.add, [bd[1], drv_r], [dn_r[q]])
                actf(dn[:, q, 0:N], dn[:, q, 0:N], AF.Ln, [dn_r[q]], [dn_r[q]])
                actf(dn[:, q, 0:N], dn[:, q, 0:N], AF.Exp, [dn_r[q]], [dn_r[q]], scale=-1.0)
                ores = hb.rsk(range(4 * pair, 4 * pair + 4), ua, ua + nq)
                tt(hb.t[:, 4 * pair:4 * pair + 4, ua:ua + nq],
                   d3, bo[0][:, 0:N].rearrange("p (a b) -> p a b", a=4), ALU.mult,
                   [bo[1], dn_r[q]], ores)

            prev = None
            pend = []
            for it in items:
                sm = scores_mm(it)
                if pend and it[2] == 1:
                    norm(*pend.pop(0))
                ex = scores_exp(it, sm)
                if prev is not None:
                    rr = pv(*prev)
                    if rr is not None:
                        pend.append(rr)
                prev = (it, ex)
            pend.append(pv(*prev))
            for p_ in pend:
                norm(*p_)
            S.op("act", lambda e: e.activation(kcar[:, j, :, 16:144], kT[:, :, 896:1024], AF.Copy),
                 reads=[kT_r[0][7], kT_r[1][7]], writes=[kcar_r[j][1]])
            S.op("act", lambda e: e.activation(vcar[:, j, 1, :], Vz[:, 7, :], AF.Copy),
                 reads=[Vz_r[7]], writes=[vcar_r[j][1]])
            out_proj_ln(wo_d[j], subtiles, hb, l * 32, l * 32 + 8)
            arena_snap[0] = S.snapshot()

        def lru(l, c, subtiles):
            j = l // 2
            if arena_snap[0] is not None:
                S.wait_snap(arena_snap[0])
            pb = 128 + j * 64
            y = arena[:, 0:4096].rearrange("p (k n) -> p k n", k=KC)
            y_r = [Res() for _ in range(KC)]
            a16 = arena[:, 4096:6144].bitcast(BF16)
            yb = a16.rearrange("p (k n) -> p k n", k=KC)
            yb_r = [Res() for _ in range(KC)]
            dgv = lambda tap, k: act.t[:, 16 + tap, k * 128:(k + 1) * 128]
            tm = arena[:, 6144:6144 + 8192].rearrange("p (s n) -> p s n", s=16)
            tm_r = [Res() for _ in range(16)]
            for tap in range(4):
                for k in range(KC):
                    S.op("dve", lambda e: e.tensor_scalar(dgv(tap, k), ident, pcol(pb + tap * 8 + k), None, ALU.mult),
                         reads=[cst_r, prm_r], writes=act.res[16 + tap])
            xk = lambda k: 8 + k
            first_c = min(st_[0] for st_ in subtiles)
            if c == 0:
                S.op("dve", lambda e: e.memset(act.t[:, 8:16, first_c - 3:first_c], 0.0),
                     writes=act.rsk(range(8, 16), first_c - 3, first_c))
                S.op("dve", lambda e: e.memset(hst[:, j, :], 0.0), writes=hst_r[j])
            else:
                S.op("dve", lambda e: e.tensor_copy(act.t[:, 8:16, first_c - 3:first_c], chist[:, j, :, 0:3]),
                     reads=[chist_r[j]], writes=act.rsk(range(8, 16), first_c - 3, first_c))
            for pc in range(4):
                t, r = load_w(mk_proj(win_d[j], pc))
                wv = t[:, :].rearrange("p (k n) -> p k n", k=KC)
                for (c0, c1) in subtiles:
                    n = c1 - c0
                    for cc in range(4):
                        oc = pc * 4 + cc
                        bk = bank()
                        for k in range(KC):
                            mm(bk, 128, n, wv[:, k, cc * 128:(cc + 1) * 128], hb.t[:, k, c0:c1],
                               k == 0, k == KC - 1, [r] + hb.rs(k, c0, c1))
                        ar = act.rs(oc, c0, c1)
                        if oc < 8:
                            if USE_GELU_TANH:
                                actf(act.t[:, oc, c0:c1], bk[0][:, 0:n], AF.Gelu_apprx_tanh, [bk[1]], ar)
                            else:
                                actf(tm[:, 0, 0:n], bk[0][:, 0:n], AF.Square, [bk[1]], [tm_r[0]])
                                S.op("dve", lambda e: e.tensor_scalar(tm[:, 0, 0:n], tm[:, 0, 0:n], 0.044715, 1.0,
                                                                      ALU.mult, ALU.add),
                                     reads=[tm_r[0]], writes=[tm_r[0]])
                                tt(tm[:, 0, 0:n], tm[:, 0, 0:n], bk[0][:, 0:n], ALU.mult, [tm_r[0], bk[1]], [tm_r[0]])
                                actf(tm[:, 0, 0:n], tm[:, 0, 0:n], AF.Sigmoid, [tm_r[0]], [tm_r[0]], scale=1.5957691216)
                                tt(act.t[:, oc, c0:c1], tm[:, 0, 0:n], bk[0][:, 0:n], ALU.mult, [tm_r[0], bk[1]], ar)
                        else:
                            S.op("dve", lambda e: e.tensor_copy(act.t[:, oc, c0:c1], bk[0][:, 0:n]), reads=[bk[1]], writes=ar)
                reserved.clear()
            S.op("dve", lambda e: e.tensor_copy(chist[:, j, :, 0:3], act.t[:, 8:16, W - 3:W]),
                 reads=act.rsk(range(8, 16), W - 3, W), writes=[chist_r[j]])
            t, r = load_w(lambda t: [
                (t[:, 0:2048].rearrange("p (n kk d) -> p n kk d", n=4, kk=2),
                 wa_d[j].rearrange("n (kk p) d -> p n kk d", p=128)),
                (t[:, 2048:4096].rearrange("p (n kk d) -> p n kk d", n=4, kk=2),
                 wi_d[j].rearrange("n (kk p) d -> p n kk d", p=128))])
            wg = [t[:, 0:2048].rearrange("p (n kk d) -> p n kk d", n=4, kk=2),
                  t[:, 2048:4096].rearrange("p (n kk d) -> p n kk d", n=4, kk=2)]
            tcnt = [0]
            prefetch(mk_proj(wout_d[j], 0))
            prefetch(mk_proj(wout_d[j], 1))
            for (c0, c1) in sorted(subtiles):
                n = c1 - c0
                for k in range(KC):
                    bk = bank()
                    for tap in range(4):
                        mm(bk, 128, n, dgv(tap, k), act.t[:, xk(k), c0 - 3 + tap:c1 - 3 + tap],
                           tap == 0, tap == 3, act.res[16 + tap] + act.rs(xk(k), c0 - 3, c1))
                    actf(y[:, k, 0:n], bk[0][:, 0:n], AF.Identity, [bk[1], prm_r], [y_r[k]], bias=pcol(pb + 32 + k))
                    S.op("dve", lambda e: e.tensor_scalar(yb[:, k, 0:n], bk[0][:, 0:n], pcol(pb + 32 + k), None, ALU.add),
                         reads=[bk[1], prm_r], writes=[yb_r[k]])
                for kb in range(2):
                    for kq in range(4):
                        k = 4 * kb + kq
                        blk, mh = k // 2, k % 2
                        br_, bi_ = bank(), bank()
                        for kk in range(2):
                            mm(br_, 128, n, wg[0][:, blk, kk, mh * 128:(mh + 1) * 128], yb[:, 2 * blk + kk, 0:n],
                               kk == 0, kk == 1, [r, yb_r[2 * blk + kk]])
                        for kk in range(2):
                            mm(bi_, 128, n, wg[1][:, blk, kk, mh * 128:(mh + 1) * 128], yb[:, 2 * blk + kk, 0:n],
                               kk == 0, kk == 1, [r, yb_r[2 * blk + kk]])
                        A_, M_, U_ = kq, 4 + kq, 8 + kq
                        TH = 12 + (tcnt[0] % 2)
                        tcnt[0] += 1
                        actf(tm[:, TH, 0:n], br_[0][:, 0:n], AF.Tanh, [br_[1], drv_r], [tm_r[TH]],
                             bias=drv[:, 64 + j * 8 + k:64 + j * 8 + k + 1], scale=0.5)
                        actf(tm[:, A_, 0:n], tm[:, TH, 0:n], AF.Exp, [tm_r[TH], drv_r], [tm_r[A_]],
                             bias=drv[:, j * 8 + k:j * 8 + k + 1], scale=drv[:, j * 8 + k:j * 8 + k + 1])
                        S.op("pool", lambda e: e.tensor_tensor(tm[:, M_, 0:n], tm[:, A_, 0:n], tm[:, A_, 0:n], ALU.mult),
                             reads=[tm_r[A_]], writes=[tm_r[M_]])
                        actf(tm[:, U_, 0:n], bi_[0][:, 0:n], AF.Tanh, [bi_[1], drv_r], [tm_r[U_]],
                             bias=drv[:, 80 + j * 8 + k:80 + j * 8 + k + 1], scale=0.5)
                        S.op("dve", lambda e: e.scalar_tensor_tensor(tm[:, U_, 0:n], tm[:, U_, 0:n], 1.0, y[:, k, 0:n],
                                                                     ALU.add, ALU.mult),
                             reads=[tm_r[U_], y_r[k]], writes=[tm_r[U_]])
                    for kq in range(4):
                        k = 4 * kb + kq
                        A_, M_, U_ = kq, 4 + kq, 8 + kq
                        HS = 14 + (k % 2)
                        actf(tm[:, M_, 0:n], tm[:, M_, 0:n], AF.Sqrt, [tm_r[M_]], [tm_r[M_]], bias=0.25, scale=-0.25)
                        S.op("pool", lambda e: e.tensor_tensor(tm[:, U_, 0:n], tm[:, U_, 0:n], tm[:, M_, 0:n], ALU.mult),
                             reads=[tm_r[U_], tm_r[M_]], writes=[tm_r[U_]])
                        S.op("dve", lambda e: e.tensor_tensor_scan(tm[:, HS, 0:n], tm[:, A_, 0:n], tm[:, U_, 0:n],
                                                                   hst[:, j, k:k + 1], ALU.mult, ALU.add),
                             reads=[tm_r[A_], tm_r[U_], hst_r[j][k]], writes=[tm_r[HS]])
                        S.op("dve", lambda e: e.tensor_copy(hst[:, j, k:k + 1], tm[:, HS, n - 1:n]),
                             reads=[tm_r[HS]], writes=[hst_r[j][k]])
                        tt(hb.t[:, k, c0:c1], tm[:, HS, 0:n], act.t[:, k, c0:c1], ALU.mult,
                           [tm_r[HS]] + act.rs(k, c0, c1), hb.rs(k, c0, c1))
            preswitch(AF.Ln)
            out_proj_ln(wout_d[j], subtiles, hb, l * 32, l * 32 + 8)
            arena_snap[0] = S.snapshot()

        for s in range(NSEQ):
            for c in range(NCH):
                if c == 0:
                    subtiles = [(R0, R0 + 512), (R0 + 512, W), (M0, R0)]
                    cf, p0 = M0, 0
                else:
                    subtiles = [(R0, R0 + 512), (R0 + 512, W)]
                    cf, p0 = R0, NMETA + CH * c
                ncol = W - cf
                for k in range(KC):
                    S.dma("sp", ld_ds[k], [(h.t[:, k, cf:W], hin[s, k * 128:(k + 1) * 128, p0:p0 + ncol])],
                          writes=h.rs(k, cf, W))
                for (c0, c1) in subtiles:
                    for k in range(KC):
                        actf(hb.t[:, k, c0:c1], h.t[:, k, c0:c1], AF.Copy, h.rs(k, c0, c1), hb.rs(k, c0, c1))
                last_pass = (s == NSEQ - 1 and c == NCH - 1)
                for li, l in enumerate(layers):
                    kind = "attn" if l % 2 == 0 else "lru"
                    next_phase[0] = ("ffn", l)
                    if kind == "attn":
                        attention(l, c, subtiles)
                    else:
                        lru(l, c, subtiles)
                    if li + 1 < len(layers):
                        nl = layers[li + 1]
                        next_phase[0] = ("attn" if nl % 2 == 0 else "lru", nl)
                    elif not last_pass:
                        nl = layers[0]
                        next_phase[0] = ("attn" if nl % 2 == 0 else "lru", nl)
                    else:
                        next_phase[0] = None
                    ffn(l, subtiles)
                for k in range(KC):
                    S.dma("sp", st_ds[k], [(hout[s, k * 128:(k + 1) * 128, p0:p0 + ncol], h.t[:, k, cf:W])],
                          reads=h.rs(k, cf, W))
        for k in range(KC):
            S.E["sp"].eng.wait_ge(st_ds[k].sem, st_ds[k].cnt)
    return nc


def _consts():
    cst = np.zeros((128, 1920), np.float32)
    cst[:, 0:128] = np.eye(128, dtype=np.float32)
    for m in range(128):
        partner = m + 32 if (m % 64) < 32 else m - 32
        cst[partner, 128 + m] = 1.0
    cst[:, 256:384] = 1.0 / D
    cst[:, 384:448] = 1.0
    cst[:, 576:640] = 1.0
    cj = np.arange(128)[:, None]
    qi = np.arange(128)[None, :]
    mprev = np.where(cj > qi, 1.0, 0.0).astype(np.float32)
    mcur = np.where(cj <= qi, 1.0, 0.0).astype(np.float32)
    cst[:, 640:1152] = np.tile(mprev, (1, 4))
    cst[:, 1152:1664] = np.tile(mcur, (1, 4))
    cst[0:16, 1664:1728] = 1.0
    cst[0:16, 1856:1920] = 1.0
    return cst


def _rope(T):
    half = 32
    inv_freq = (np.float32(10000.0) ** (-np.arange(half, dtype=np.float32) * np.float32(2.0) / np.float32(64)))
    pos = np.arange(T, dtype=np.float32)
    ang = (pos[None, :] * inv_freq[:, None]).astype(np.float32)
    cos = np.cos(ang).astype(np.float32)
    sin = np.sin(ang).astype(np.float32)
    c64 = np.concatenate([cos, cos], 0)
    s64 = np.concatenate([-sin, sin], 0)
    out = np.zeros((128, 2, T), np.float32)
    out[:, 0, :] = np.concatenate([c64, c64], 0)
    out[:, 1, :] = np.concatenate([s64, s64], 0)
    return out


def _vec(v):
    return np.ascontiguousarray(np.asarray(v, np.float32).reshape(8, 128).T)


def _prep_shared(inp):
    prm = np.zeros((128, 272), np.float32)
    for l in range(DEPTH):
        prm[:, l * 32 + 0:l * 32 + 8] = _vec(inp["ln_mix_g"][l])
        prm[:, l * 32 + 8:l * 32 + 16] = _vec(inp["ln_mix_b"][l])
        prm[:, l * 32 + 16:l * 32 + 24] = _vec(inp["ln_ffn_g"][l])
        prm[:, l * 32 + 24:l * 32 + 32] = _vec(inp["ln_ffn_b"][l])
    for j in range(2):
        b = 128 + j * 64
        for tap in range(4):
            prm[:, b + tap * 8:b + tap * 8 + 8] = _vec(inp["lru_conv_w"][j, tap])
        prm[:, b + 32:b + 40] = _vec(inp["lru_conv_b"][j])
        prm[:, b + 40:b + 48] = _vec(inp["lru_b_a"][j].reshape(-1))
        prm[:, b + 48:b + 56] = _vec(inp["lru_b_i"][j].reshape(-1))
        prm[:, b + 56:b + 64] = _vec(inp["lru_lambda"][j])
        for pair in range(2):
            for jq in range(4):
                ch = 4 * pair + jq
                prm[0:64, 256 + j * 8 + pair * 4 + jq] = inp["attn_sinks"][j, ORDER[2 * ch]]
                prm[64:128, 256 + j * 8 + pair * 4 + jq] = inp["attn_sinks"][j, ORDER[2 * ch + 1]]
    qcols = np.concatenate([np.arange(hh * 64, hh * 64 + 64) for hh in ORDER])
    wqkv = np.asarray(inp["attn_w_qkv"], np.float32)
    shared = {
        "prm": prm, "cst": _consts(),
        "wgu": np.ascontiguousarray(inp["ffn_w_gate_up"], dtype=np.float32),
        "wdn": np.ascontiguousarray(inp["ffn_w_down"], dtype=np.float32),
        "wq": np.ascontiguousarray(wqkv[:, :, qcols]),
        "wkv": np.ascontiguousarray(wqkv[:, :, 1024:1536]),
        "wo": np.ascontiguousarray(np.asarray(inp["attn_w_o"], np.float32)[:, qcols, :]),
        "win": np.ascontiguousarray(inp["lru_w_in"], dtype=np.float32),
        "wa": np.ascontiguousarray(inp["lru_w_a"], dtype=np.float32),
        "wi": np.ascontiguousarray(inp["lru_w_i"], dtype=np.float32),
        "wout": np.ascontiguousarray(inp["lru_w_out"], dtype=np.float32),
    }
    return shared


def run_layers(hT_per_core, shared, layers, NSEQ, SEQ, ncores, trace=False):
    nc = build(NSEQ, SEQ, layers)
    need = {"prm", "cst", "rope", "wgu", "wdn"}
    if any(l % 2 == 0 for l in layers):
        need |= {"wq", "wkv", "wo"}
    if any(l % 2 == 1 for l in layers):
        need |= {"win", "wa", "wi", "wout"}
    in_maps = []
    for ci in range(ncores):
        m = {k: v for k, v in shared.items() if k in need}
        m["hin"] = hT_per_core[ci]
        in_maps.append(m)
    res = run_bass_kernel_spmd(nc, in_maps, core_ids=list(range(ncores)), **({"trace": True} if trace else {}))
    return [r["hout"] for r in res.results], res


LAYER_GROUPS = [[0, 1, 2, 3]]


def kernel(**inp):
    x = np.asarray(inp["x"], np.float32)
    B, SEQ, _ = x.shape
    ncores = 8
    NSEQ = B // ncores
    T = NMETA + SEQ
    shared = _prep_shared(inp)
    shared["rope"] = _rope(T)
    meta = np.asarray(inp["meta_tokens"], np.float32)
    hT = []
    for ci in range(ncores):
        a = np.empty((NSEQ, D, T), np.float32)
        for s in range(NSEQ):
            a[s, :, :NMETA] = meta.T
            a[s, :, NMETA:] = x[ci * NSEQ + s].T
        hT.append(a)
    for grp in LAYER_GROUPS:
        hT, _ = run_layers(hT, shared, grp, NSEQ, SEQ, ncores)
    out = np.empty((B, SEQ, D), np.float32)
    for ci in range(ncores):
        for s in range(NSEQ):
            out[ci * NSEQ + s] = hT[ci][s][:, NMETA:].T
    return out
```

```python
import numpy as np
from contextlib import ExitStack
import concourse.bass as bass
import concourse.mybir as mybir
from concourse.bass_utils import run_bass_kernel_spmd

F32 = mybir.dt.float32
BF16 = mybir.dt.bfloat16
AF = mybir.ActivationFunctionType
ALU = mybir.AluOpType

D = 1024
KC = 8
DFF = 2816
FC = 22
NMETA = 16
DEPTH = 4
ALPHA = (2.0 * DEPTH) ** 0.25
EPS = 1e-5
NEG = -30000.0
OFF = 4
M0 = OFF
R0 = OFF + NMETA
CH = 1024
W = R0 + CH
ORDER = [0, 4, 1, 5, 2, 6, 3, 7, 8, 12, 9, 13, 10, 14, 11, 15]
USE_SILU = True
USE_GELU_TANH = True
SKIP_MIXER = False
SKIP_FFN = False
ATT_STOP = 9


class Res:
    __slots__ = ("w", "r", "psum")

    def __init__(self, psum=False):
        self.w = None
        self.r = {}
        self.psum = psum


class Eng:
    def __init__(self, name, eng, sem):
        self.name, self.eng, self.sem, self.cnt, self.seen = name, eng, sem, 0, {}


class DSem:
    def __init__(self, key, sem):
        self.key, self.sem, self.cnt = key, sem, 0


def units(c0, c1):
    us = []
    if c0 < R0:
        us.append(0)
    for j in range(1, 9):
        a = R0 + 128 * (j - 1)
        if c0 < a + 128 and c1 > a:
            us.append(j)
    return us


class Buf:
    def __init__(self, t, nk):
        self.t = t
        self.res = [[Res() for _ in range(9)] for _ in range(nk)]

    def rs(self, k, c0, c1):
        return [self.res[k][u] for u in units(c0, c1)]

    def rsk(self, ks, c0, c1):
        return [self.res[k][u] for k in ks for u in units(c0, c1)]


class Sched:
    def __init__(self, nc, es):
        self.nc = nc
        self.es = es
        self.E = {}
        for n, a in [("pe", "tensor"), ("act", "scalar"), ("dve", "vector"), ("pool", "gpsimd"), ("sp", "sync")]:
            self.E[n] = Eng(n, getattr(nc, a), es.enter_context(nc.semaphore("s_" + n)))
        self.nds = 0

    def dsem(self):
        self.nds += 1
        return DSem("d%d" % self.nds, self.es.enter_context(self.nc.semaphore("d%d" % self.nds)))

    def _deps(self, e, reads, writes):
        best = {}
        for r in reads:
            t = r.w
            if t is not None and (t[0] not in best or best[t[0]][2] < t[2]):
                best[t[0]] = t
            if r.psum:
                for t in r.r.values():
                    if t[0] != e.name and (t[0] not in best or best[t[0]][2] < t[2]):
                        best[t[0]] = t
        for w in writes:
            t = w.w
            if t is not None and (t[0] not in best or best[t[0]][2] < t[2]):
                best[t[0]] = t
            for t in w.r.values():
                if t[0] not in best or best[t[0]][2] < t[2]:
                    best[t[0]] = t
        for t in best.values():
            if e.name == "pe" and t[0] == "pe":
                continue
            if e.seen.get(t[0], 0) >= t[2]:
                continue
            e.eng.wait_ge(t[1], t[2])
            e.seen[t[0]] = t[2]

    @staticmethod
    def _mark(tok, reads, writes):
        for r in reads:
            o = r.r.get(tok[0])
            if o is None or o[2] < tok[2]:
                r.r[tok[0]] = tok
        for w in writes:
            w.w = tok
            w.r = {}

    def op(self, en, fn, reads=(), writes=(), inc=True):
        e = self.E[en]
        self._deps(e, reads, writes)
        ins = fn(e.eng)
        if inc:
            ins.then_inc(e.sem, 1)
            e.cnt += 1
            tok = (en, e.sem, e.cnt)
        else:
            tok = (en, e.sem, e.cnt + 1)
        self._mark(tok, reads, writes)
        return ins

    def dma(self, qn, ds, parts, reads=(), writes=()):
        e = self.E[qn]
        self._deps(e, reads, writes)
        for dst, src in parts:
            e.eng.dma_start(out=dst, in_=src).then_inc(ds.sem, 16)
            ds.cnt += 16
        tok = (ds.key, ds.sem, ds.cnt)
        self._mark(tok, reads, writes)

    def snapshot(self):
        return {n: e.cnt for n, e in self.E.items()}

    def wait_snap(self, snap):
        for e in self.E.values():
            for n2, c in snap.items():
                e2 = self.E[n2]
                if e2 is e or c == 0 or e.seen.get(n2, 0) >= c:
                    continue
                e.eng.wait_ge(e2.sem, c)
                e.seen[n2] = c

    def barrier(self):
        for e in self.E.values():
            for e2 in self.E.values():
                if e2 is e or e2.cnt == 0:
                    continue
                if e.seen.get(e2.name, 0) >= e2.cnt:
                    continue
                e.eng.wait_ge(e2.sem, e2.cnt)
                e.seen[e2.name] = e2.cnt


def build(NSEQ, SEQ, layers):
    T = NMETA + SEQ
    NCH = SEQ // CH
    attn_js = sorted({l // 2 for l in layers if l % 2 == 0})
    lru_js = sorted({l // 2 for l in layers if l % 2 == 1})
    nc = bass.Bass("TRN2", target_bir_lowering=False)
    dt = lambda name, shape: nc.dram_tensor(name, shape, F32, kind="ExternalInput").ap()
    hin = dt("hin", [NSEQ, D, T])
    hout = nc.dram_tensor("hout", [NSEQ, D, T], F32, kind="ExternalOutput").ap()
    prm_d = dt("prm", [128, 272])
    cst_d = dt("cst", [128, 1920])
    rope_d = dt("rope", [128, 2, T])
    wgu_d = dt("wgu", [DEPTH, D, 2 * DFF])
    wdn_d = dt("wdn", [DEPTH, DFF, D])
    if attn_js:
        wq_d = dt("wq", [2, D, 1024])
        wkv_d = dt("wkv", [2, D, 512])
        wo_d = dt("wo", [2, D, D])
    if lru_js:
        win_d = dt("win", [2, D, 2048])
        wa_d = dt("wa", [2, 4, 256, 256])
        wi_d = dt("wi", [2, 4, 256, 256])
        wout_d = dt("wout", [2, D, D])

    with ExitStack() as es:
        S = Sched(nc, es)
        sb = lambda name, shape, d=F32: es.enter_context(nc.sbuf_tensor(name, shape, d))
        h = Buf(sb("h", [128, KC, W]), KC)
        hb = Buf(sb("hb", [128, KC, W], BF16), KC)
        act = Buf(sb("actb", [128, FC, W], BF16), FC)
        NSLOT = 3
        slots = []
        for i in range(NSLOT):
            slots.append((sb("wslot%d" % i, [128, 4096], BF16), Res(), S.dsem()))
        slot_i = [0]
        prm = sb("prm_s", [128, 272])
        prm_r = Res()
        cst = sb("cst_s", [128, 1920], BF16)
        cst_r = Res()
        ident = cst[:, 0:128]
        perm = cst[:, 128:256]
        onesD = cst[:, 256:384]
        onesh = [cst[:, 384:512], cst[:, 512:640]]
        maskb = [cst[:, 640:1152], cst[:, 1152:1664]]
        ones16 = [cst[:, 1664:1792], cst[:, 1792:1920]]
        drv = sb("drv", [128, 128])
        drv_r = Res()
        lnz = sb("lnz", [128, 2, 512])
        lnz_r = [Res(), Res()]
        lnst = sb("lnst", [128, 2, 512])
        lnst_r = [Res(), Res()]
        lnsq = sb("lnsq", [128, 4, 512], BF16)
        lnsq_r = [Res() for _ in range(4)]
        sgt, sgt_r = lnz, lnz_r
        xbr = sb("xbr", [128, 4, 512], BF16)
        xbr_r = [Res() for _ in range(4)]
        kcar = sb("kcar", [128, 2, 2, 144], BF16)
        kcar_r = [[Res(), Res()] for _ in range(2)]
        vcar = sb("vcar", [128, 2, 2, 512], BF16)
        vcar_r = [[Res(), Res()] for _ in range(2)]
        hst = sb("hst", [128, 2, KC])
        hst_r = [[Res() for _ in range(KC)] for _ in range(2)]
        chist = sb("chist", [128, 2, KC, 4], BF16)
        chist_r = [Res(), Res()]
        ARW = 14336
        arena = sb("arena", [128, ARW])
        banks = [(es.enter_context(nc.psum_tensor("ps%d" % i, [128, 512], F32)), Res(True), i) for i in range(8)]
        reserved = set()
        bank_i = [0]
        arena_snap = [None]
        ld_ds = [S.dsem() for _ in range(KC)]
        st_ds = [S.dsem() for _ in range(KC)]
        c_ds = S.dsem()
        rp_ds = S.dsem()

        def bank():
            while True:
                bank_i[0] = (bank_i[0] + 1) % 8
                if bank_i[0] not in reserved:
                    return banks[bank_i[0]]

        def bank_reserve():
            bk = bank()
            reserved.add(bk[2])
            return bk

        prefetched = []

        def issue_load(mk):
            i = slot_i[0]
            slot_i[0] = (i + 1) % NSLOT
            t, r, ds = slots[i]
            S.dma("pool", ds, mk(t), writes=[r])
            return t, r

        def load_w(mk):
            if prefetched:
                return prefetched.pop(0)
            return issue_load(mk)

        def prefetch(mk):
            prefetched.append(issue_load(mk))

        def mk_proj(w_ap, pc):
            return lambda t: [(t[:, :].rearrange("p (k n) -> p k n", k=KC),
                               w_ap[:, pc * 512:(pc + 1) * 512].rearrange("(k p) n -> p k n", p=128))]

        def mk_gu(l, pc):
            j0 = 2 * pc
            return lambda t: [
                (t[:, 0:2048].rearrange("p (k n) -> p k n", k=KC),
                 wgu_d[l, :, j0 * 128:(j0 + 2) * 128].rearrange("(k p) n -> p k n", p=128)),
                (t[:, 2048:4096].rearrange("p (k n) -> p k n", k=KC),
                 wgu_d[l, :, DFF + j0 * 128:DFF + (j0 + 2) * 128].rearrange("(k p) n -> p k n", p=128))]

        def first_pieces(kind, l):
            if kind == "attn":
                return [mk_proj(wq_d[l // 2], 0), mk_proj(wq_d[l // 2], 1)]
            if kind == "lru":
                return [mk_proj(win_d[l // 2], 0), mk_proj(win_d[l // 2], 1)]
            return [mk_gu(l, 0), mk_gu(l, 1)]

        next_phase = [None]

        def mm(bk, m, n, lhsT, rhs, start, stop, reads, inc=None):
            S.op("pe", lambda e: e.matmul(bk[0][0:m, 0:n], lhsT, rhs, start=start, stop=stop),
                 reads=reads, writes=[bk[1]], inc=(stop if inc is None else inc))

        def actf(out, in_, func, reads, writes, bias=None, scale=None):
            kw = {}
            if bias is not None:
                kw["bias"] = bias
            if scale is not None:
                kw["scale"] = scale
            S.op("act", lambda e: e.activation(out, in_, func, **kw), reads=reads, writes=writes)

        def tt(out, a, b, op, reads, writes):
            S.op("dve", lambda e: e.tensor_tensor(out, a, b, op), reads=reads, writes=writes)

        def pcol(i):
            return prm[:, i:i + 1]

        S.dma("sp", c_ds, [(prm[:], prm_d)], writes=[prm_r])
        c2_ds = S.dsem()
        S.dma("pool", c2_ds, [(cst[:], cst_d)], writes=[cst_r])
        for j in lru_js:
            lam = prm[:, 128 + j * 64 + 56:128 + j * 64 + 64]
            tmp = drv[:, 48 + j * 8:56 + j * 8]
            actf(tmp, lam, AF.Exp, [prm_r], [drv_r], scale=-1.0)
            actf(tmp, tmp, AF.Ln, [drv_r], [drv_r], bias=1.0)
            S.op("dve", lambda e: e.tensor_scalar(drv[:, j * 8:j * 8 + 8], tmp, -4.0, None, ALU.mult),
                 reads=[drv_r], writes=[drv_r])
            S.op("dve", lambda e: e.tensor_scalar(drv[:, 16 + j * 8:24 + j * 8], tmp, -8.0, None, ALU.mult),
                 reads=[drv_r], writes=[drv_r])
            S.op("dve", lambda e: e.tensor_scalar(drv[:, 64 + j * 8:72 + j * 8], prm[:, 128 + j * 64 + 40:128 + j * 64 + 48],
                                                  0.5, None, ALU.mult), reads=[prm_r, drv_r], writes=[drv_r])
            S.op("dve", lambda e: e.tensor_scalar(drv[:, 80 + j * 8:88 + j * 8], prm[:, 128 + j * 64 + 48:128 + j * 64 + 56],
                                                  0.5, None, ALU.mult), reads=[prm_r, drv_r], writes=[drv_r])
        for j in attn_js:
            actf(drv[:, 32 + j * 8:40 + j * 8], prm[:, 256 + j * 8:264 + j * 8], AF.Exp, [prm_r], [drv_r])

        tsw = sb("tsw", [128, 4])
        tsw_r = Res()
        S.op("dve", lambda e: e.memset(tsw[:, :], 0.0), writes=[tsw_r])

        def preswitch(func):
            kw = {"bias": 1.0} if func == AF.Ln else {}
            S.op("act", lambda e: e.activation(tsw[:, 1:2], tsw[:, 0:1], func, **kw), reads=[tsw_r], writes=[tsw_r])

        def ln_s1(c0, c1):
            n = c1 - c0
            bm, be = bank(), bank()
            for k in range(KC):
                hr = h.rs(k, c0, c1)
                hbr = hb.rs(k, c0, c1)
                actf(hb.t[:, k, c0:c1], h.t[:, k, c0:c1], AF.Copy, hr, hbr)
                q = k % 4
                actf(lnsq[:, q, 0:n], h.t[:, k, c0:c1], AF.Square, hr, [lnsq_r[q]])
                mm(bm, 128, n, onesD, hb.t[:, k, c0:c1], k == 0, k == KC - 1, hbr + [cst_r])
                mm(be, 128, n, onesD, lnsq[:, q, 0:n], k == 0, k == KC - 1, [lnsq_r[q], cst_r], inc=True)
            return (c0, c1, bm, be)

        def ln_s2(st, q):
            c0, c1, bm, be = st
            n = c1 - c0
            actf(lnst[:, q, 0:n], bm[0][:, 0:n], AF.Square, [bm[1]], [lnst_r[q]])
            tt(lnst[:, q, 0:n], be[0][:, 0:n], lnst[:, q, 0:n], ALU.subtract, [be[1], lnst_r[q]], [lnst_r[q]])
            actf(lnst[:, q, 0:n], lnst[:, q, 0:n], AF.Ln, [lnst_r[q]], [lnst_r[q]], bias=EPS)
            actf(be[0][:, 0:n], lnst[:, q, 0:n], AF.Exp, [lnst_r[q]], [be[1]], scale=-0.5)

        def preswitch_next():
            if next_phase[0] is None:
                return
            kind = next_phase[0][0]
            if kind == "ffn":
                preswitch(AF.Silu)
            elif kind == "lru":
                preswitch(AF.Gelu_apprx_tanh)

        def ln_s3_k(st, gi, bi, k, eng="act"):
            c0, c1, bm, br = st
            n = c1 - c0
            hr = h.rs(k, c0, c1)
            hbr = hb.rs(k, c0, c1)
            q = k % 2
            tt(lnz[:, q, 0:n], h.t[:, k, c0:c1], bm[0][:, 0:n], ALU.subtract, hr + [bm[1]], [lnz_r[q]])
            tt(lnz[:, q, 0:n], lnz[:, q, 0:n], br[0][:, 0:n], ALU.mult, [lnz_r[q], br[1]], [lnz_r[q]])
            if eng == "act":
                actf(h.t[:, k, c0:c1], lnz[:, q, 0:n], AF.Identity, [lnz_r[q], prm_r], hr,
                     bias=pcol(bi + k), scale=pcol(gi + k))
                actf(hb.t[:, k, c0:c1], lnz[:, q, 0:n], AF.Identity, [lnz_r[q], prm_r], hbr,
                     bias=pcol(bi + k), scale=pcol(gi + k))
            elif eng == "mix":
                actf(hb.t[:, k, c0:c1], lnz[:, q, 0:n], AF.Identity, [lnz_r[q], prm_r], hbr,
                     bias=pcol(bi + k), scale=pcol(gi + k))
                S.op("pool", lambda e: e.tensor_scalar(h.t[:, k, c0:c1], lnz[:, q, 0:n], pcol(gi + k), pcol(bi + k),
                                                       ALU.mult, ALU.add), reads=[lnz_r[q], prm_r], writes=hr)
            else:
                S.op("pool", lambda e: e.tensor_scalar(h.t[:, k, c0:c1], lnz[:, q, 0:n], pcol(gi + k), pcol(bi + k),
                                                       ALU.mult, ALU.add), reads=[lnz_r[q], prm_r], writes=hr)
                S.op("pool", lambda e: e.tensor_scalar(hb.t[:, k, c0:c1], lnz[:, q, 0:n], pcol(gi + k), pcol(bi + k),
                                                       ALU.mult, ALU.add), reads=[lnz_r[q], prm_r], writes=hbr)

        def ln_s3(st, gi, bi, split=KC):
            for k in range(KC):
                ln_s3_k(st, gi, bi, k, "act" if k < split else "pool")

        class StatAcc:
            def __init__(self, subtiles, delay):
                reserved.clear()
                self.big = [st for st in subtiles if st[1] - st[0] > 16]
                self.bk = {st: (bank_reserve(), bank_reserve()) for st in self.big}
                self.cnt = {st: 0 for st in self.big}
                self.q = []
                self.delay = delay
                self.ri = 0

            def produced(self, m, c0, c1):
                if (c0, c1) not in self.bk:
                    return
                n = c1 - c0
                sl = self.ri % 4
                self.ri += 1
                hr = h.rs(m, c0, c1)
                actf(xbr[:, sl, 0:n], h.t[:, m, c0:c1], AF.Copy, hr, [xbr_r[sl]])
                actf(lnsq[:, sl, 0:n], h.t[:, m, c0:c1], AF.Square, hr, [lnsq_r[sl]])
                self.q.append((c0, c1, sl))
                self.flush(self.delay)

            def flush(self, keep):
                while len(self.q) > keep:
                    c0, c1, sl = self.q.pop(0)
                    n = c1 - c0
                    bm, be = self.bk[(c0, c1)]
                    i = self.cnt[(c0, c1)]
                    self.cnt[(c0, c1)] = i + 1
                    mm(bm, 128, n, onesD, xbr[:, sl, 0:n], i == 0, i == KC - 1, [xbr_r[sl], cst_r], inc=True)
                    mm(be, 128, n, onesD, lnsq[:, sl, 0:n], i == 0, i == KC - 1, [lnsq_r[sl], cst_r], inc=True)

            def flush_st(self, st):
                rest = [x for x in self.q if (x[0], x[1]) != st]
                mine = [x for x in self.q if (x[0], x[1]) == st]
                self.q = mine
                self.flush(0)
                self.q = rest
                return (st[0], st[1], self.bk[st][0], self.bk[st][1])

            def finish(self):
                self.flush(0)
                return [(c0, c1, self.bk[(c0, c1)][0], self.bk[(c0, c1)][1]) for (c0, c1) in self.big]

        def layer_norm_all(subtiles, gi, bi, acc=None):
            if next_phase[0] is not None:
                for mk in first_pieces(*next_phase[0]):
                    prefetch(mk)
            big = [st for st in subtiles if st[1] - st[0] > 16]
            small = [st for st in subtiles if st[1] - st[0] <= 16]
            sts = acc.finish() if acc is not None else None
            if sts is None:
                sts = [ln_s1(c0, c1) for (c0, c1) in big]
            for i, st in enumerate(sts):
                ln_s2(st, i % 2)
            if not small:
                preswitch_next()
            for st in sts[:-1]:
                ln_s3(st, gi, bi, split=8)
            ln_s3(sts[-1], gi, bi, split=0)
            for (c0, c1) in small:
                st = ln_s1(c0, c1)
                ln_s2(st, 0)
                ln_s3(st, gi, bi)
            if small:
                preswitch_next()
            reserved.clear()
            reserved.add(sts[-1][2][2])
            reserved.add(sts[-1][3][2])

        def out_proj_ln(w_d, subtiles, src, gi, bi):
            acc = StatAcc(subtiles, 2)
            pcs = [load_w(mk_proj(w_d, 0)), load_w(mk_proj(w_d, 1))]
            big = [st for st in subtiles if st[1] - st[0] > 16]
            small = [st for st in subtiles if st[1] - st[0] <= 16]
            prev = None
            prev_eng = "act"
            for si, (c0, c1) in enumerate(big + small):
                n = c1 - c0
                for m in range(KC):
                    t, r = pcs[m // 4]
                    wv = t[:, :].rearrange("p (k n) -> p k n", k=KC)
                    mmi = m % 4
                    bk = bank()
                    for k in range(KC):
                        mm(bk, 128, n, wv[:, k, mmi * 128:(mmi + 1) * 128], src.t[:, k, c0:c1],
                           k == 0, k == KC - 1, [r] + src.rs(k, c0, c1))
                    hr = h.rs(m, c0, c1)
                    S.op("dve", lambda e: e.scalar_tensor_tensor(h.t[:, m, c0:c1], h.t[:, m, c0:c1], ALPHA,
                                                                 bk[0][:, 0:n], ALU.mult, ALU.add),
                         reads=hr + [bk[1]], writes=hr)
                    acc.produced(m, c0, c1)
                    if prev is not None:
                        ln_s3_k(prev, gi, bi, m, prev_eng)
                if (c0, c1) in big:
                    st = acc.flush_st((c0, c1))
                    ln_s2(st, si % 2)
                    last_big = (c0, c1) == big[-1]
                    if last_big and not small:
                        preswitch_next()
                    if last_big and next_phase[0] is not None:
                        fp = first_pieces(*next_phase[0])
                        prefetch(fp[0])
                        if not small:
                            prefetch(fp[1])
                    if last_big and not small:
                        for k_ in range(KC):
                            ln_s3_k(st, gi, bi, k_, "mix")
                        prev = None
                    else:
                        prev = st
                        prev_eng = "pool" if last_big else "act"
                else:
                    prev = None
                    st = ln_s1(c0, c1)
                    ln_s2(st, 0)
                    ln_s3(st, gi, bi)
                    preswitch_next()
            if small and next_phase[0] is not None:
                prefetch(first_pieces(*next_phase[0])[1])
            lb = big[-1]
            reserved.clear()
            reserved.add(acc.bk[lb][0][2])
            reserved.add(acc.bk[lb][1][2])


        def ffn(l, subtiles):
            for pc in range(FC // 2):
                j0 = 2 * pc
                t, r = load_w(mk_gu(l, pc))
                wg = t[:, 0:2048].rearrange("p (k n) -> p k n", k=KC)
                wu = t[:, 2048:4096].rearrange("p (k n) -> p k n", k=KC)
                for (c0, c1) in subtiles:
                    n = c1 - c0
                    for jj in range(2):
                        j = j0 + jj
                        bg, bu = bank(), bank()
                        for k in range(KC):
                            mm(bg, 128, n, wg[:, k, jj * 128:(jj + 1) * 128], hb.t[:, k, c0:c1],
                               k == 0, k == KC - 1, [r] + hb.rs(k, c0, c1))
                        for k in range(KC):
                            mm(bu, 128, n, wu[:, k, jj * 128:(jj + 1) * 128], hb.t[:, k, c0:c1],
                               k == 0, k == KC - 1, [r] + hb.rs(k, c0, c1))
                        q = j % 2
                        ar = act.rs(j, c0, c1)
                        if USE_SILU:
                            actf(sgt[:, q, 0:n], bg[0][:, 0:n], AF.Silu, [bg[1]], [sgt_r[q]])
                        else:
                            actf(sgt[:, q, 0:n], bg[0][:, 0:n], AF.Sigmoid, [bg[1]], [sgt_r[q]])
                            tt(sgt[:, q, 0:n], sgt[:, q, 0:n], bg[0][:, 0:n], ALU.mult, [sgt_r[q], bg[1]], [sgt_r[q]])
                        tt(act.t[:, j, c0:c1], sgt[:, q, 0:n], bu[0][:, 0:n], ALU.mult, [sgt_r[q], bu[1]], ar)
                reserved.clear()
            preswitch(AF.Ln)
            acc = StatAcc(subtiles, 1)
            for m in range(KC):
                t, r = load_w(lambda t: [(t[:, 0:FC * 128].rearrange("p (j n) -> p j n", j=FC),
                                          wdn_d[l, :, m * 128:(m + 1) * 128].rearrange("(j p) n -> p j n", p=128))])
                wd = t[:, 0:FC * 128].rearrange("p (j n) -> p j n", j=FC)
                for (c0, c1) in subtiles:
                    n = c1 - c0
                    bk = bank()
                    for j in range(FC):
                        mm(bk, 128, n, wd[:, j, :], act.t[:, j, c0:c1], j == 0, j == FC - 1,
                           [r] + act.rs(j, c0, c1))
                    hr = h.rs(m, c0, c1)
                    S.op("dve", lambda e: e.scalar_tensor_tensor(h.t[:, m, c0:c1], h.t[:, m, c0:c1], ALPHA,
                                                                 bk[0][:, 0:n], ALU.mult, ALU.add),
                         reads=hr + [bk[1]], writes=hr)
                    acc.produced(m, c0, c1)
            layer_norm_all(subtiles, l * 32 + 16, l * 32 + 24, acc)

        def attention(l, c, subtiles):
            j = l // 2
            if arena_snap[0] is not None:
                S.wait_snap(arena_snap[0])
            a16 = arena[:, 0:6400].bitcast(BF16)
            kT = a16[:, 0:2048].rearrange("p (k n) -> p k n", k=2)
            kT_r = [[Res() for _ in range(8)] for _ in range(2)]
            Vz = a16[:, 2048:6144].rearrange("p (b n) -> p b n", b=8)
            Vz_r = [Res() for _ in range(8)]
            Et = a16[:, 6144:9216].rearrange("p (s n) -> p s n", s=6)
            Et_r = [Res() for _ in range(6)]
            qb = a16[:, 9216:10240].rearrange("p (s n) -> p s n", s=2)
            qb_r = [Res(), Res()]
            t12 = arena[:, 6400:8448].rearrange("p (s n) -> p s n", s=4)
            t12_r = [Res() for _ in range(4)]
            rp = arena[:, 8448:8448 + 2 * W].rearrange("p (s n) -> p s n", s=2)
            rp_r = Res()
            dn = arena[:, 10560:11584].rearrange("p (s n) -> p s n", s=2)
            dn_r = [Res(), Res()]
            cfirst = min(st_[0] for st_ in subtiles)
            pos0 = 0 if c == 0 else NMETA + CH * c
            ncol = W - cfirst
            S.dma("sp", rp_ds, [(rp[:, 0, cfirst:W], rope_d[:, 0, pos0:pos0 + ncol]),
                                 (rp[:, 1, cfirst:W], rope_d[:, 1, pos0:pos0 + ncol])], writes=[rp_r])
            S.op("pool", lambda e: e.memset(arena[:, 1024:3072], 0.0), writes=Vz_r)
            S.op("pool", lambda e: e.memset(arena[:, 3072:4608], 0.0), writes=Et_r)
            if c == 0:
                S.op("dve", lambda e: e.memset(vcar[:, j, :, :], 0.0), writes=vcar_r[j])
                S.op("dve", lambda e: e.memset(kcar[:, j, :, :], 0.0), writes=kcar_r[j])
            if ATT_STOP <= 0:
                S.op('dve', lambda e: e.tensor_copy(t12[:, 0, 0:16], rp[:, 0, cfirst:cfirst + 16]), reads=[rp_r], writes=[t12_r[0]])
                return
            rcnt = [0]

            def rope_out(bk, n, c0, c1, out_ap, out_res):
                q = rcnt[0] % 2
                rcnt[0] += 1
                actf(qb[:, q, 0:n], bk[0][:, 0:n], AF.Copy, [], [qb_r[q], bk[1]])
                b2 = bank()
                mm(b2, 128, n, perm, qb[:, q, 0:n], True, True, [qb_r[q], cst_r])
                tt(t12[:, 2 * q, 0:n], rp[:, 0, c0:c1], bk[0][:, 0:n], ALU.mult, [bk[1], rp_r], [t12_r[2 * q]])
                tt(t12[:, 2 * q + 1, 0:n], rp[:, 1, c0:c1], b2[0][:, 0:n], ALU.mult, [b2[1], rp_r], [t12_r[2 * q + 1]])
                tt(out_ap, t12[:, 2 * q, 0:n], t12[:, 2 * q + 1, 0:n], ALU.add,
                   [t12_r[2 * q], t12_r[2 * q + 1]], out_res)

            for pc in range(2):
                t, r = load_w(mk_proj(wq_d[j], pc))
                wv = t[:, :].rearrange("p (k n) -> p k n", k=KC)
                for (c0, c1) in subtiles:
                    n = c1 - c0
                    for cc in range(4):
                        ch = pc * 4 + cc
                        bk = bank()
                        for k in range(KC):
                            mm(bk, 128, n, wv[:, k, cc * 128:(cc + 1) * 128], hb.t[:, k, c0:c1],
                               k == 0, k == KC - 1, [r] + hb.rs(k, c0, c1))
                        rope_out(bk, n, c0, c1, act.t[:, ch, c0:c1], act.rs(ch, c0, c1))
                reserved.clear()
            if ATT_STOP <= 1:
                return
            t, r = load_w(lambda t: [(t[:, :].rearrange("p (k n) -> p k n", k=KC),
                                      wkv_d[j].rearrange("(k p) n -> p k n", p=128))])
            wv = t[:, :].rearrange("p (k n) -> p k n", k=KC)
            for (c0, c1) in subtiles:
                n = c1 - c0
                meta = c0 < R0
                for kc in range(2):
                    bk = bank()
                    for k in range(KC):
                        mm(bk, 128, n, wv[:, k, kc * 128:(kc + 1) * 128], hb.t[:, k, c0:c1],
                           k == 0, k == KC - 1, [r] + hb.rs(k, c0, c1))
                    if meta:
                        rope_out(bk, n, c0, c1, kcar[:, j, kc, 0:16], [kcar_r[j][0]])
                    else:
                        b0 = (c0 - R0) // 128
                        rope_out(bk, n, c0, c1, kT[:, kc, c0 - R0:c1 - R0],
                                 [kT_r[kc][b] for b in range(b0, b0 + n // 128)])
                for u in units(c0, c1):
                    if u == 0:
                        ua, nt = M0, 16
                    else:
                        ua, nt = R0 + 128 * (u - 1), 128
                    bk = bank()
                    for k in range(KC):
                        mm(bk, 128, 256, hb.t[:, k, ua:ua + 128], wv[:, k, 256:512], k == 0, k == KC - 1,
                           [r] + hb.rs(k, ua, ua + 128))
                    if u == 0:
                        dst, dres = vcar[0:16, j, 0, :], [vcar_r[j][0]]
                    else:
                        dst, dres = Vz[:, u - 1, :], [Vz_r[u - 1]]
                    src = bk[0][0:nt, 0:256].rearrange("p (g d) -> p g d", g=4)
                    d4 = dst.rearrange("p (g x) -> p g x", g=4)
                    actf(d4[:, 0::2, 0:64], src[:, 0::2, :], AF.Copy, [bk[1]], dres)
                    actf(d4[:, 1::2, 64:128], src[:, 1::2, :], AF.Copy, [bk[1]], dres)

            if ATT_STOP <= 2:
                return
            prefetch(mk_proj(wo_d[j], 0))
            prefetch(mk_proj(wo_d[j], 1))
            items = []
            ulist = ([0] if c == 0 else []) + list(range(1, 9))
            for u in ulist:
                for pair in range(2):
                    for half in range(2):
                        items.append((u, pair, half))

            def keygroups(u, pair, half):
                g = 2 * pair + half
                lo, hi = 64 * half, 64 * half + 64
                if u == 0:
                    return [(kcar[lo:hi, j, pair, 0:128], kcar_r[j], vcar[:, j, 0, g * 128:(g + 1) * 128],
                             [vcar_r[j][0]], 16, 1)]
                b = u - 1
                kg = [(kcar[lo:hi, j, pair, 0:128], kcar_r[j], vcar[:, j, 0, g * 128:(g + 1) * 128],
                       [vcar_r[j][0]], 16, None)]
                if b == 0:
                    if c > 0:
                        kg.append((kcar[lo:hi, j, pair, 16:144], [kcar_r[j][1]],
                                   vcar[:, j, 1, g * 128:(g + 1) * 128], [vcar_r[j][1]], 128, 0))
                else:
                    kg.append((kT[lo:hi, pair, (b - 1) * 128:b * 128], [kT_r[pair][b - 1]],
                               Vz[:, b - 1, g * 128:(g + 1) * 128], [Vz_r[b - 1]], 128, 0))
                kg.append((kT[lo:hi, pair, b * 128:(b + 1) * 128], [kT_r[pair][b]],
                           Vz[:, b, g * 128:(g + 1) * 128], [Vz_r[b]], 128, 1))
                return kg

            ecnt = [0]
            mcnt = [0]

            def scores_mm(item):
                u, pair, half = item
                if u == 0:
                    ua, nq = M0, 16
                else:
                    ua, nq = R0 + 128 * (u - 1), 128
                N = 4 * nq
                lo, hi = 64 * half, 64 * half + 64
                qap = act.t[lo:hi, 4 * pair:4 * pair + 4, ua:ua + nq]
                qres = act.rsk(range(4 * pair, 4 * pair + 4), ua, ua + nq)
                out = []
                for (kap, kres, vap, vres, nk, mk) in keygroups(u, pair, half):
                    bk = bank()
                    mm(bk, 128, N, kap, qap, True, True, kres + qres)
                    out.append((bk, nk, mk, vap, vres))
                return out

            def scores_exp(item, sm):
                u, pair, half = item
                nq = 16 if u == 0 else 128
                N = 4 * nq
                out = []
                for (bk, nk, mk, vap, vres) in sm:
                    if nk == 16:
                        s = 4 + mcnt[0] % 2
                        mcnt[0] += 1
                    else:
                        s = ecnt[0] % 4
                        ecnt[0] += 1
                    actf(Et[0:nk, s, 0:N], bk[0][0:nk, 0:N], AF.Exp, [bk[1]], [Et_r[s]], scale=0.125)
                    if mk is not None:
                        mrhs = maskb[mk][0:nk, :].rearrange("p (a b) -> p a b", a=4)[:, :, 0:nq]
                        e3 = Et[0:nk, s, 0:N].rearrange("p (a b) -> p a b", a=4)
                        tt(e3, e3, mrhs, ALU.mult, [Et_r[s], cst_r], [Et_r[s]])
                    out.append((s, nk, vap, vres))
                return out

            pv_state = {}

            def pv(item, ex):
                u, pair, half = item
                if u == 0:
                    ua, nq = M0, 16
                else:
                    ua, nq = R0 + 128 * (u - 1), 128
                N = 4 * nq
                if half == 0:
                    pv_state["o"], pv_state["d"] = bank(), bank()
                bo, bd = pv_state["o"], pv_state["d"]
                for i, (s, nk, vap, vres) in enumerate(ex):
                    first = (half == 0 and i == 0)
                    last = (half == 1 and i == len(ex) - 1)
                    mm(bo, 128, N, vap, Et[:, s, 0:N], first, last, vres + [Et_r[s]])
                    mm(bd, 128, N, (ones16 if nk == 16 else onesh)[half], Et[:, s, 0:N], first, last, [cst_r, Et_r[s]],
                       inc=(i == len(ex) - 1))
                if half == 1:
                    return (u, pair, bo, bd)
                return None

            ncnt = [0]

            def norm(u, pair, bo, bd):
                if u == 0:
                    ua, nq = M0, 16
                else:
                    ua, nq = R0 + 128 * (u - 1), 128
                N = 4 * nq
                q = ncnt[0] % 2
                ncnt[0] += 1
                sk = drv[:, 32 + j * 8 + pair * 4:32 + j * 8 + pair * 4 + 4]
                skb = bass.AP(sk.tensor, sk.offset, [list(sk.ap[0]), [1, 4], [0, nq]])
                d3 = dn[:, q, 0:N].rearrange("p (a b) -> p a b", a=4)
                tt(d3, skb, bd[0][:, 0:N].rearrange("p (a b) -> p a b", a=4), ALU.add, [bd[1], drv_r], [dn_r[q]])
                actf(dn[:, q, 0:N], dn[:, q, 0:N], AF.Ln, [dn_r[q]], [dn_r[q]])
                actf(dn[:, q, 0:N], dn[:, q, 0:N], AF.Exp, [dn_r[q]], [dn_r[q]], scale=-1.0)
                ores = hb.rsk(range(4 * pair, 4 * pair + 4), ua, ua + nq)
                tt(hb.t[:, 4 * pair:4 * pair + 4, ua:ua + nq],
                   d3, bo[0][:, 0:N].rearrange("p (a b) -> p a b", a=4), ALU.mult,
                   [bo[1], dn_r[q]], ores)

            prev = None
            pend = []
            for it in items:
                sm = scores_mm(it)
                if pend and it[2] == 1:
                    norm(*pend.pop(0))
                ex = scores_exp(it, sm)
                if prev is not None:
                    rr = pv(*prev)
                    if rr is not None:
                        pend.append(rr)
                prev = (it, ex)
            pend.append(pv(*prev))
            for p_ in pend:
                norm(*p_)
            S.op("act", lambda e: e.activation(kcar[:, j, :, 16:144], kT[:, :, 896:1024], AF.Copy),
                 reads=[kT_r[0][7], kT_r[1][7]], writes=[kcar_r[j][1]])
            S.op("act", lambda e: e.activation(vcar[:, j, 1, :], Vz[:, 7, :], AF.Copy),
                 reads=[Vz_r[7]], writes=[vcar_r[j][1]])
            out_proj_ln(wo_d[j], subtiles, hb, l * 32, l * 32 + 8)
            arena_snap[0] = S.snapshot()

        def lru(l, c, subtiles):
            j = l // 2
            if arena_snap[0] is not None:
                S.wait_snap(arena_snap[0])
            pb = 128 + j * 64
            y = arena[:, 0:4096].rearrange("p (k n) -> p k n", k=KC)
            y_r = [Res() for _ in range(KC)]
            a16 = arena[:, 4096:6144].bitcast(BF16)
            yb = a16.rearrange("p (k n) -> p k n", k=KC)
            yb_r = [Res() for _ in range(KC)]
            dgv = lambda tap, k: act.t[:, 16 + tap, k * 128:(k + 1) * 128]
            tm = arena[:, 6144:6144 + 8192].rearrange("p (s n) -> p s n", s=16)
            tm_r = [Res() for _ in range(16)]
            for tap in range(4):
                for k in range(KC):
                    S.op("dve", lambda e: e.tensor_scalar(dgv(tap, k), ident, pcol(pb + tap * 8 + k), None, ALU.mult),
                         reads=[cst_r, prm_r], writes=act.res[16 + tap])
            xk = lambda k: 8 + k
            first_c = min(st_[0] for st_ in subtiles)
            if c == 0:
                S.op("dve", lambda e: e.memset(act.t[:, 8:16, first_c - 3:first_c], 0.0),
                     writes=act.rsk(range(8, 16), first_c - 3, first_c))
                S.op("dve", lambda e: e.memset(hst[:, j, :], 0.0), writes=hst_r[j])
            else:
                S.op("dve", lambda e: e.tensor_copy(act.t[:, 8:16, first_c - 3:first_c], chist[:, j, :, 0:3]),
                     reads=[chist_r[j]], writes=act.rsk(range(8, 16), first_c - 3, first_c))
            for pc in range(4):
                t, r = load_w(mk_proj(win_d[j], pc))
                wv = t[:, :].rearrange("p (k n) -> p k n", k=KC)
                for (c0, c1) in subtiles:
                    n = c1 - c0
                    for cc in range(4):
                        oc = pc * 4 + cc
                        bk = bank()
                        for k in range(KC):
                            mm(bk, 128, n, wv[:, k, cc * 128:(cc + 1) * 128], hb.t[:, k, c0:c1],
                               k == 0, k == KC - 1, [r] + hb.rs(k, c0, c1))
                        ar = act.rs(oc, c0, c1)
                        if oc < 8:
                            if USE_GELU_TANH:
                                actf(act.t[:, oc, c0:c1], bk[0][:, 0:n], AF.Gelu_apprx_tanh, [bk[1]], ar)
                            else:
                                actf(tm[:, 0, 0:n], bk[0][:, 0:n], AF.Square, [bk[1]], [tm_r[0]])
                                S.op("dve", lambda e: e.tensor_scalar(tm[:, 0, 0:n], tm[:, 0, 0:n], 0.044715, 1.0,
                                                                      ALU.mult, ALU.add),
                                     reads=[tm_r[0]], writes=[tm_r[0]])
                                tt(tm[:, 0, 0:n], tm[:, 0, 0:n], bk[0][:, 0:n], ALU.mult, [tm_r[0], bk[1]], [tm_r[0]])
                                actf(tm[:, 0, 0:n], tm[:, 0, 0:n], AF.Sigmoid, [tm_r[0]], [tm_r[0]], scale=1.5957691216)
                                tt(act.t[:, oc, c0:c1], tm[:, 0, 0:n], bk[0][:, 0:n], ALU.mult, [tm_r[0], bk[1]], ar)
                        else:
                            S.op("dve", lambda e: e.tensor_copy(act.t[:, oc, c0:c1], bk[0][:, 0:n]), reads=[bk[1]], writes=ar)
                reserved.clear()
            S.op("dve", lambda e: e.tensor_copy(chist[:, j, :, 0:3], act.t[:, 8:16, W - 3:W]),
                 reads=act.rsk(range(8, 16), W - 3, W), writes=[chist_r[j]])
            t, r = load_w(lambda t: [
                (t[:, 0:2048].rearrange("p (n kk d) -> p n kk d", n=4, kk=2),
                 wa_d[j].rearrange("n (kk p) d -> p n kk d", p=128)),
                (t[:, 2048:4096].rearrange("p (n kk d) -> p n kk d", n=4, kk=2),
                 wi_d[j].rearrange("n (kk p) d -> p n kk d", p=128))])
            wg = [t[:, 0:2048].rearrange("p (n kk d) -> p n kk d", n=4, kk=2),
                  t[:, 2048:4096].rearrange("p (n kk d) -> p n kk d", n=4, kk=2)]
            tcnt = [0]
            prefetch(mk_proj(wout_d[j], 0))
            prefetch(mk_proj(wout_d[j], 1))
            for (c0, c1) in sorted(subtiles):
                n = c1 - c0
                for k in range(KC):
                    bk = bank()
                    for tap in range(4):
                        mm(bk, 128, n, dgv(tap, k), act.t[:, xk(k), c0 - 3 + tap:c1 - 3 + tap],
                           tap == 0, tap == 3, act.res[16 + tap] + act.rs(xk(k), c0 - 3, c1))
                    actf(y[:, k, 0:n], bk[0][:, 0:n], AF.Identity, [bk[1], prm_r], [y_r[k]], bias=pcol(pb + 32 + k))
                    S.op("dve", lambda e: e.tensor_scalar(yb[:, k, 0:n], bk[0][:, 0:n], pcol(pb + 32 + k), None, ALU.add),
                         reads=[bk[1], prm_r], writes=[yb_r[k]])
                for kb in range(2):
                    for kq in range(4):
                        k = 4 * kb + kq
                        blk, mh = k // 2, k % 2
                        br_, bi_ = bank(), bank()
                        for kk in range(2):
                            mm(br_, 128, n, wg[0][:, blk, kk, mh * 128:(mh + 1) * 128], yb[:, 2 * blk + kk, 0:n],
                               kk == 0, kk == 1, [r, yb_r[2 * blk + kk]])
                        for kk in range(2):
                            mm(bi_, 128, n, wg[1][:, blk, kk, mh * 128:(mh + 1) * 128], yb[:, 2 * blk + kk, 0:n],
                               kk == 0, kk == 1, [r, yb_r[2 * blk + kk]])
                        A_, M_, U_ = kq, 4 + kq, 8 + kq
                        TH = 12 + (tcnt[0] % 2)
                        tcnt[0] += 1
                        actf(tm[:, TH, 0:n], br_[0][:, 0:n], AF.Tanh, [br_[1], drv_r], [tm_r[TH]],
                             bias=drv[:, 64 + j * 8 + k:64 + j * 8 + k + 1], scale=0.5)
                        actf(tm[:, A_, 0:n], tm[:, TH, 0:n], AF.Exp, [tm_r[TH], drv_r], [tm_r[A_]],
                             bias=drv[:, j * 8 + k:j * 8 + k + 1], scale=drv[:, j * 8 + k:j * 8 + k + 1])
                        S.op("pool", lambda e: e.tensor_tensor(tm[:, M_, 0:n], tm[:, A_, 0:n], tm[:, A_, 0:n], ALU.mult),
                             reads=[tm_r[A_]], writes=[tm_r[M_]])
                        actf(tm[:, U_, 0:n], bi_[0][:, 0:n], AF.Tanh, [bi_[1], drv_r], [tm_r[U_]],
                             bias=drv[:, 80 + j * 8 + k:80 + j * 8 + k + 1], scale=0.5)
                        S.op("dve", lambda e: e.scalar_tensor_tensor(tm[:, U_, 0:n], tm[:, U_, 0:n], 1.0, y[:, k, 0:n],
                                                                     ALU.add, ALU.mult),
                             reads=[tm_r[U_], y_r[k]], writes=[tm_r[U_]])
                    for kq in range(4):
                        k = 4 * kb + kq
                        A_, M_, U_ = kq, 4 + kq, 8 + kq
                        HS = 14 + (k % 2)
                        actf(tm[:, M_, 0:n], tm[:, M_, 0:n], AF.Sqrt, [tm_r[M_]], [tm_r[M_]], bias=0.25, scale=-0.25)
                        S.op("pool", lambda e: e.tensor_tensor(tm[:, U_, 0:n], tm[:, U_, 0:n], tm[:, M_, 0:n], ALU.mult),
                             reads=[tm_r[U_], tm_r[M_]], writes=[tm_r[U_]])
                        S.op("dve", lambda e: e.tensor_tensor_scan(tm[:, HS, 0:n], tm[:, A_, 0:n], tm[:, U_, 0:n],
                                                                   hst[:, j, k:k + 1], ALU.mult, ALU.add),
                             reads=[tm_r[A_], tm_r[U_], hst_r[j][k]], writes=[tm_r[HS]])
                        S.op("dve", lambda e: e.tensor_copy(hst[:, j, k:k + 1], tm[:, HS, n - 1:n]),
                             reads=[tm_r[HS]], writes=[hst_r[j][k]])
                        tt(hb.t[:, k, c0:c1], tm[:, HS, 0:n], act.t[:, k, c0:c1], ALU.mult,
                           [tm_r[HS]] + act.rs(k, c0, c1), hb.rs(k, c0, c1))
            preswitch(AF.Ln)
            out_proj_ln(wout_d[j], subtiles, hb, l * 32, l * 32 + 8)
            arena_snap[0] = S.snapshot()

        for s in range(NSEQ):
            for c in range(NCH):
                if c == 0:
                    subtiles = [(R0, R0 + 512), (R0 + 512, W), (M0, R0)]
                    cf, p0 = M0, 0
                else:
                    subtiles = [(R0, R0 + 512), (R0 + 512, W)]
                    cf, p0 = R0, NMETA + CH * c
                ncol = W - cf
                for k in range(KC):
                    S.dma("sp", ld_ds[k], [(h.t[:, k, cf:W], hin[s, k * 128:(k + 1) * 128, p0:p0 + ncol])],
                          writes=h.rs(k, cf, W))
                for (c0, c1) in subtiles:
                    for k in range(KC):
                        actf(hb.t[:, k, c0:c1], h.t[:, k, c0:c1], AF.Copy, h.rs(k, c0, c1), hb.rs(k, c0, c1))
                last_pass = (s == NSEQ - 1 and c == NCH - 1)
                for li, l in enumerate(layers):
                    kind = "attn" if l % 2 == 0 else "lru"
                    next_phase[0] = ("ffn", l)
                    if kind == "attn":
                        attention(l, c, subtiles)
                    else:
                        lru(l, c, subtiles)
                    if li + 1 < len(layers):
                        nl = layers[li + 1]
                        next_phase[0] = ("attn" if nl % 2 == 0 else "lru", nl)
                    elif not last_pass:
                        nl = layers[0]
                        next_phase[0] = ("attn" if nl % 2 == 0 else "lru", nl)
                    else:
                        next_phase[0] = None
                    ffn(l, subtiles)
                for k in range(KC):
                    S.dma("sp", st_ds[k], [(hout[s, k * 128:(k + 1) * 128, p0:p0 + ncol], h.t[:, k, cf:W])],
                          reads=h.rs(k, cf, W))
        for k in range(KC):
            S.E["sp"].eng.wait_ge(st_ds[k].sem, st_ds[k].cnt)
    return nc


def _consts():
    cst = np.zeros((128, 1920), np.float32)
    cst[:, 0:128] = np.eye(128, dtype=np.float32)
    for m in range(128):
        partner = m + 32 if (m % 64) < 32 else m - 32
        cst[partner, 128 + m] = 1.0
    cst[:, 256:384] = 1.0 / D
    cst[:, 384:448] = 1.0
    cst[:, 576:640] = 1.0
    cj = np.arange(128)[:, None]
    qi = np.arange(128)[None, :]
    mprev = np.where(cj > qi, 1.0, 0.0).astype(np.float32)
    mcur = np.where(cj <= qi, 1.0, 0.0).astype(np.float32)
    cst[:, 640:1152] = np.tile(mprev, (1, 4))
    cst[:, 1152:1664] = np.tile(mcur, (1, 4))
    cst[0:16, 1664:1728] = 1.0
    cst[0:16, 1856:1920] = 1.0
    return cst


def _rope(T):
    half = 32
    inv_freq = (np.float32(10000.0) ** (-np.arange(half, dtype=np.float32) * np.float32(2.0) / np.float32(64)))
    pos = np.arange(T, dtype=np.float32)
    ang = (pos[None, :] * inv_freq[:, None]).astype(np.float32)
    cos = np.cos(ang).astype(np.float32)
    sin = np.sin(ang).astype(np.float32)
    c64 = np.concatenate([cos, cos], 0)
    s64 = np.concatenate([-sin, sin], 0)
    out = np.zeros((128, 2, T), np.float32)
    out[:, 0, :] = np.concatenate([c64, c64], 0)
    out[:, 1, :] = np.concatenate([s64, s64], 0)
    return out


def _vec(v):
    return np.ascontiguousarray(np.asarray(v, np.float32).reshape(8, 128).T)


def _prep_shared(inp):
    prm = np.zeros((128, 272), np.float32)
    for l in range(DEPTH):
        prm[:, l * 32 + 0:l * 32 + 8] = _vec(inp["ln_mix_g"][l])
        prm[:, l * 32 + 8:l * 32 + 16] = _vec(inp["ln_mix_b"][l])
        prm[:, l * 32 + 16:l * 32 + 24] = _vec(inp["ln_ffn_g"][l])
        prm[:, l * 32 + 24:l * 32 + 32] = _vec(inp["ln_ffn_b"][l])
    for j in range(2):
        b = 128 + j * 64
        for tap in range(4):
            prm[:, b + tap * 8:b + tap * 8 + 8] = _vec(inp["lru_conv_w"][j, tap])
        prm[:, b + 32:b + 40] = _vec(inp["lru_conv_b"][j])
        prm[:, b + 40:b + 48] = _vec(inp["lru_b_a"][j].reshape(-1))
        prm[:, b + 48:b + 56] = _vec(inp["lru_b_i"][j].reshape(-1))
        prm[:, b + 56:b + 64] = _vec(inp["lru_lambda"][j])
        for pair in range(2):
            for jq in range(4):
                ch = 4 * pair + jq
                prm[0:64, 256 + j * 8 + pair * 4 + jq] = inp["attn_sinks"][j, ORDER[2 * ch]]
                prm[64:128, 256 + j * 8 + pair * 4 + jq] = inp["attn_sinks"][j, ORDER[2 * ch + 1]]
    qcols = np.concatenate([np.arange(hh * 64, hh * 64 + 64) for hh in ORDER])
    wqkv = np.asarray(inp["attn_w_qkv"], np.float32)
    shared = {
        "prm": prm, "cst": _consts(),
        "wgu": np.ascontiguousarray(inp["ffn_w_gate_up"], dtype=np.float32),
        "wdn": np.ascontiguousarray(inp["ffn_w_down"], dtype=np.float32),
        "wq": np.ascontiguousarray(wqkv[:, :, qcols]),
        "wkv": np.ascontiguousarray(wqkv[:, :, 1024:1536]),
        "wo": np.ascontiguousarray(np.asarray(inp["attn_w_o"], np.float32)[:, qcols, :]),
        "win": np.ascontiguousarray(inp["lru_w_in"], dtype=np.float32),
        "wa": np.ascontiguousarray(inp["lru_w_a"], dtype=np.float32),
        "wi": np.ascontiguousarray(inp["lru_w_i"], dtype=np.float32),
        "wout": np.ascontiguousarray(inp["lru_w_out"], dtype=np.float32),
    }
    return shared


def run_layers(hT_per_core, shared, layers, NSEQ, SEQ, ncores, trace=False):
    nc = build(NSEQ, SEQ, layers)
    need = {"prm", "cst", "rope", "wgu", "wdn"}
    if any(l % 2 == 0 for l in layers):
        need |= {"wq", "wkv", "wo"}
    if any(l % 2 == 1 for l in layers):
        need |= {"win", "wa", "wi", "wout"}
    in_maps = []
    for ci in range(ncores):
        m = {k: v for k, v in shared.items() if k in need}
        m["hin"] = hT_per_core[ci]
        in_maps.append(m)
    res = run_bass_kernel_spmd(nc, in_maps, core_ids=list(range(ncores)), **({"trace": True} if trace else {}))
    return [r["hout"] for r in res.results], res


LAYER_GROUPS = [[0, 1, 2, 3]]


def kernel(**inp):
    x = np.asarray(inp["x"], np.float32)
    B, SEQ, _ = x.shape
    ncores = 8
    NSEQ = B // ncores
    T = NMETA + SEQ
    shared = _prep_shared(inp)
    shared["rope"] = _rope(T)
    meta = np.asarray(inp["meta_tokens"], np.float32)
    hT = []
    for ci in range(ncores):
        a = np.empty((NSEQ, D, T), np.float32)
        for s in range(NSEQ):
            a[s, :, :NMETA] = meta.T
            a[s, :, NMETA:] = x[ci * NSEQ + s].T
        hT.append(a)
    for grp in LAYER_GROUPS:
        hT, _ = run_layers(hT, shared, grp, NSEQ, SEQ, ncores)
    out = np.empty((B, SEQ, D), np.float32)
    for ci in range(ncores):
        for s in range(NSEQ):
            out[ci * NSEQ + s] = hT[ci][s][:, NMETA:].T
    return out
```
